# Optimizing a Trainium2 kernel written in Bass

```python
import jax, jax.numpy as jnp
from jax import lax
import numpy as np

D_MODEL = 2048
BATCH = 4
SEQ = 2048
DEPTH = 4
DEC_BATCH = 128
DEC_SEQ = 4
PAST_LEN = 16384
PAGE_SIZE = 128

N_META = 16
D_A = D_MODEL // 2
D_B = D_MODEL // 2
K_A = 31
K_B = 3
D_IN_EVEN = 2 * D_A + 3 * D_B
POOL_WINDOWS = (2, 4, 8, 16)
N_POOL_GROUPS = 4
D_POOL = D_MODEL // N_POOL_GROUPS
POOL_BUF = 15
D_FF = 4 * D_MODEL
N_EVEN = (DEPTH + 1) // 2
N_ODD = DEPTH // 2
EPS = 1e-6

kernel_name = "hybrid_conv_pool_decoder_step"


def rmsnorm(x, g):
    xf = x.astype(jnp.float32)
    y = xf * lax.rsqrt(jnp.mean(xf * xf, axis=-1, keepdims=True) + EPS)
    return (y * g.astype(jnp.float32)).astype(x.dtype)


def layernorm(x, g, b):
    xf = x.astype(jnp.float32)
    mu = jnp.mean(xf, axis=-1, keepdims=True)
    xc = xf - mu
    y = xc * lax.rsqrt(jnp.mean(xc * xc, axis=-1, keepdims=True) + EPS)
    return (y * g.astype(jnp.float32) + b.astype(jnp.float32)).astype(x.dtype)


def causal_dwconv(x_ext, w):
    c = x_ext.shape[-1]
    return lax.conv_general_dilated(
        x_ext, w[:, None, :].astype(x_ext.dtype), window_strides=(1,), padding='VALID',
        dimension_numbers=('NWC', 'WIO', 'NWC'), feature_group_count=c)


def even_mixer(h, buf_a, buf_b, w_in, conv_a_w, conv_a_b, ln_g, ln_b, conv_b_w, w_out):
    proj = jnp.einsum('nld,de->nle', h, w_in)
    a_val, a_gate, b_gate, c_gate, b_x = jnp.split(
        proj, [D_A, 2 * D_A, 2 * D_A + D_B, 2 * D_A + 2 * D_B], axis=-1)
    a = a_val * jax.nn.sigmoid(a_gate)
    a_ext = jnp.concatenate([buf_a, a], axis=1)
    a_conv = causal_dwconv(a_ext, conv_a_w) + conv_a_b.astype(a.dtype)
    a_out = jax.nn.silu(layernorm(a_conv, ln_g, ln_b))
    v = c_gate * b_x
    v_ext = jnp.concatenate([buf_b, v], axis=1)
    b_out = b_gate * causal_dwconv(v_ext, conv_b_w)
    y = jnp.einsum('nle,ed->nld', jnp.concatenate([a_out, b_out], axis=-1), w_out)
    return y, a_ext[:, -(K_A - 1):], v_ext[:, -(K_B - 1):]


def pool_mixer(h, buf, pos, w_groups, scale):
    l = h.shape[1]
    u_ext = jnp.concatenate([buf, h], axis=1)
    cs = jnp.cumsum(u_ext.astype(jnp.float32), axis=1)
    cs = jnp.pad(cs, ((0, 0), (1, 0), (0, 0)))
    end = cs[:, POOL_BUF + 1:]
    hf = h.astype(jnp.float32)
    outs = []
    for g, w in enumerate(POOL_WINDOWS):
        sl = slice(g * D_POOL, (g + 1) * D_POOL)
        start = cs[:, POOL_BUF + 1 - w:POOL_BUF + 1 - w + l, sl]
        cnt = jnp.minimum(pos + 1, w).astype(jnp.float32)[None, :, None]
        pooled = (end[..., sl] - start) / cnt - hf[..., sl]
        outs.append(jnp.einsum('nlc,cd->nld', pooled, w_groups[g].astype(jnp.float32)))
    y = jnp.concatenate(outs, axis=-1) * scale.astype(jnp.float32)
    return y.astype(h.dtype), u_ext[:, -POOL_BUF:]


def squared_relu_mlp(h, w_up, w_down):
    return jnp.einsum('nlf,fd->nld', jnp.square(jax.nn.relu(jnp.einsum('nld,df->nlf', h, w_up))), w_down)


def trunk(x, pos, bufs_a, bufs_b, bufs_p, norm_mix, norm_mlp, norm_final, w_in_even,
          conv_a_w, conv_a_b, ln_a_g, ln_a_b, conv_b_w, w_out_even, pool_w, pool_scale,
          w_mlp_up, w_mlp_down):
    new_a, new_b, new_p = [], [], []
    for layer in range(DEPTH):
        h = rmsnorm(x, norm_mix[layer])
        if layer % 2 == 0:
            e = layer // 2
            y, na, nb = even_mixer(h, bufs_a[e], bufs_b[e], w_in_even[e], conv_a_w[e], conv_a_b[e],
                                   ln_a_g[e], ln_a_b[e], conv_b_w[e], w_out_even[e])
            new_a.append(na)
            new_b.append(nb)
        else:
            o = layer // 2
            y, npool = pool_mixer(h, bufs_p[o], pos, pool_w[o], pool_scale[o])
            new_p.append(npool)
        x = x + y
        x = x + squared_relu_mlp(rmsnorm(x, norm_mlp[layer]), w_mlp_up[layer], w_mlp_down[layer])
    x = rmsnorm(x, norm_final)
    return x, jnp.stack(new_a), jnp.stack(new_b), jnp.stack(new_p)


def setup_inputs(seed: int = 0) -> dict:
    key = jax.random.key(seed)
    ks = jax.random.split(key, 20)

    def nrm(k, shape, scale):
        return jax.random.normal(k, shape, jnp.float32) * scale

    return {
        "x_prompt": nrm(ks[0], (BATCH, SEQ, D_MODEL), 1.0),
        "x_sample": nrm(ks[1], (DEC_BATCH, DEC_SEQ, D_MODEL), 1.0),
        "state_conv_a": nrm(ks[2], (N_EVEN, DEC_BATCH, K_A - 1, D_A), 0.5),
        "state_conv_b": nrm(ks[3], (N_EVEN, DEC_BATCH, K_B - 1, D_B), 0.5),
        "state_pool": nrm(ks[4], (N_ODD, DEC_BATCH, POOL_BUF, D_MODEL), 1.0),
        "meta_tokens": nrm(ks[5], (N_META, D_MODEL), 1.0),
        "norm_mix": 1.0 + nrm(ks[6], (DEPTH, D_MODEL), 0.02),
        "norm_mlp": 1.0 + nrm(ks[7], (DEPTH, D_MODEL), 0.02),
        "norm_final": 1.0 + nrm(ks[8], (D_MODEL,), 0.02),
        "w_in_even": nrm(ks[9], (N_EVEN, D_MODEL, D_IN_EVEN), D_MODEL ** -0.5),
        "conv_a_w": nrm(ks[10], (N_EVEN, K_A, D_A), K_A ** -0.5),
        "conv_a_b": nrm(ks[11], (N_EVEN, D_A), 0.02),
        "ln_a_g": 1.0 + nrm(ks[12], (N_EVEN, D_A), 0.02),
        "ln_a_b": nrm(ks[13], (N_EVEN, D_A), 0.02),
        "conv_b_w": nrm(ks[14], (N_EVEN, K_B, D_B), K_B ** -0.5),
        "w_out_even": nrm(ks[15], (N_EVEN, D_A + D_B, D_MODEL), (D_A + D_B) ** -0.5),
        "pool_w": nrm(ks[16], (N_ODD, N_POOL_GROUPS, D_POOL, D_POOL), D_POOL ** -0.5),
        "pool_scale": 1.0 + nrm(ks[17], (N_ODD, D_MODEL), 0.02),
        "w_mlp_up": nrm(ks[18], (DEPTH, D_MODEL, D_FF), D_MODEL ** -0.5),
        "w_mlp_down": nrm(ks[19], (DEPTH, D_FF, D_MODEL), D_FF ** -0.5),
    }


def reference(x_prompt, x_sample, state_conv_a, state_conv_b, state_pool, meta_tokens,
              norm_mix, norm_mlp, norm_final, w_in_even, conv_a_w, conv_a_b, ln_a_g, ln_a_b,
              conv_b_w, w_out_even, pool_w, pool_scale, w_mlp_up, w_mlp_down):
    params = (norm_mix, norm_mlp, norm_final, w_in_even, conv_a_w, conv_a_b, ln_a_g, ln_a_b,
              conv_b_w, w_out_even, pool_w, pool_scale, w_mlp_up, w_mlp_down)
    dt = x_prompt.dtype
    meta = jnp.broadcast_to(meta_tokens.astype(dt)[None], (BATCH, N_META, D_MODEL))
    xp = jnp.concatenate([meta, x_prompt], axis=1)
    pos_p = jnp.arange(N_META + SEQ, dtype=jnp.int32)
    zeros_a = [jnp.zeros((BATCH, K_A - 1, D_A), dt) for _ in range(N_EVEN)]
    zeros_b = [jnp.zeros((BATCH, K_B - 1, D_B), dt) for _ in range(N_EVEN)]
    zeros_p = [jnp.zeros((BATCH, POOL_BUF, D_MODEL), dt) for _ in range(N_ODD)]
    yp, new_conv_a_prompt, new_conv_b_prompt, new_pool_prompt = trunk(
        xp, pos_p, zeros_a, zeros_b, zeros_p, *params)
    y_prompt = yp[:, N_META:]
    pos_s = PAST_LEN + jnp.arange(DEC_SEQ, dtype=jnp.int32)
    bufs_a = [state_conv_a[e] for e in range(N_EVEN)]
    bufs_b = [state_conv_b[e] for e in range(N_EVEN)]
    bufs_p = [state_pool[o] for o in range(N_ODD)]
    y_sample, new_conv_a_sample, new_conv_b_sample, new_pool_sample = trunk(
        x_sample, pos_s, bufs_a, bufs_b, bufs_p, *params)
    return (y_prompt, y_sample, new_conv_a_prompt, new_conv_b_prompt, new_pool_prompt,
            new_conv_a_sample, new_conv_b_sample, new_pool_sample)
```

```python
import os
import contextlib
import numpy as np
import concourse.bass as bass
import concourse.mybir as mybir
from concourse.bass_utils import run_bass_kernel_spmd

F32 = mybir.dt.float32
BF16 = mybir.dt.bfloat16
AF = mybir.ActivationFunctionType
ALU = mybir.AluOpType

D = 2048
KC = 16
T = 1152
TT = 384
NTT = 3
PP = 1077
HALO = 90
NS = 16
LS = 4
S0 = PP
SE = PP + NS * LS
DA = 1024
DFF = 8192
EPS = 1e-6
NW = 4
NLANE = 8
SAME_SYNC_ENGS = set(os.environ.get("K_SAME_SYNC", "act,pool,dve").split(","))


class Op:
    __slots__ = ("eng", "fn", "deps", "signal", "sigval", "lane", "laneval", "idx")

    def __init__(self, eng, fn):
        self.eng = eng
        self.fn = fn
        self.deps = set()
        self.signal = False
        self.sigval = 0
        self.lane = None
        self.laneval = 0
        self.idx = 0


class Prog:
    ENGS = ("pe", "act", "dve", "pool", "sp")

    def __init__(self):
        self.ops = []
        self.acc = {}
        self.lane_cnt = {}

    def add(self, eng, fn, reads=(), writes=(), lane=None):
        op = Op(eng, fn)
        op.idx = len(self.ops)
        deps = op.deps
        for (k, lo, hi) in reads:
            a = self.acc.get(k)
            if a is None:
                continue
            for (l2, h2, p) in a[0]:
                if l2 < hi and lo < h2:
                    deps.add(p)
        for (k, lo, hi) in writes:
            a = self.acc.get(k)
            if a is None:
                continue
            for (l2, h2, p) in a[0]:
                if l2 < hi and lo < h2:
                    deps.add(p)
            for (l2, h2, p) in a[1]:
                if l2 < hi and lo < h2:
                    deps.add(p)
        isdma = lane is not None
        for (k, lo, hi) in reads:
            a = self.acc.setdefault(k, [[], []])
            if not isdma:
                a[1] = [e for e in a[1] if not (e[2].lane is None and e[2].eng == eng and lo <= e[0] and e[1] <= hi)]
            a[1].append((lo, hi, op))
        for (k, lo, hi) in writes:
            a = self.acc.setdefault(k, [[], []])
            a[0] = [e for e in a[0] if not (lo <= e[0] and e[1] <= hi)]
            a[1] = [e for e in a[1] if not (lo <= e[0] and e[1] <= hi)]
            a[0].append((lo, hi, op))
        if isdma:
            n = self.lane_cnt.get(lane, 0) + 1
            self.lane_cnt[lane] = n
            op.lane = lane
            op.laneval = 16 * n
        self.ops.append(op)
        return op

    def finalize(self):
        for op in self.ops:
            for d in op.deps:
                if d.lane is not None:
                    continue
                if d.eng == op.eng:
                    if op.eng == "pe":
                        continue
                    if not (op.eng in SAME_SYNC_ENGS or op.lane is not None):
                        continue
                d.signal = True
        cnt = {e: 0 for e in self.ENGS}
        for op in self.ops:
            if op.signal:
                cnt[op.eng] += 1
                op.sigval = cnt[op.eng]

    def schedule(self):
        per = {e: [o for o in self.ops if o.eng == e] for e in self.ENGS}
        final_lanes = dict(self.lane_cnt)
        acts = {}
        for ename in self.ENGS:
            lst = []
            waited = {}

            def wait(sem, val):
                if waited.get(sem, 0) < val:
                    lst.append(("wait", sem, val))
                    waited[sem] = val

            for op in per[ename]:
                if op.lane is not None and op.laneval > 16:
                    wait("L" + op.lane, op.laneval - 16)
                for d in sorted(op.deps, key=lambda o: o.idx):
                    if d.lane is not None:
                        wait("L" + d.lane, d.laneval)
                    else:
                        if d.eng == ename:
                            if ename == "pe":
                                continue
                            if not (ename in SAME_SYNC_ENGS or op.lane is not None):
                                continue
                        wait("E" + d.eng, d.sigval)
                if op.lane is not None:
                    lst.append(("op", op, "L" + op.lane, 16))
                elif op.signal:
                    lst.append(("op", op, "E" + ename, 1))
                else:
                    lst.append(("op", op, None, 0))
            if ename == "sp":
                for ln, n in final_lanes.items():
                    wait("L" + ln, 16 * n)
                for en in self.ENGS:
                    last = [o for o in per[en] if o.signal]
                    if last and en != ename:
                        wait("E" + en, last[-1].sigval)
            acts[ename] = lst
        return acts

    def simulate(self, acts):
        semv = {}
        pc = {e: 0 for e in self.ENGS}
        progress = True
        while progress:
            progress = False
            for e in self.ENGS:
                lst = acts[e]
                while pc[e] < len(lst):
                    a = lst[pc[e]]
                    if a[0] == "wait":
                        if semv.get(a[1], 0) < a[2]:
                            break
                    else:
                        if a[2] is not None:
                            semv[a[2]] = semv.get(a[2], 0) + a[3]
                    pc[e] += 1
                    progress = True
        stuck = {e: (pc[e], len(acts[e]), acts[e][pc[e]] if pc[e] < len(acts[e]) else None) for e in self.ENGS}
        ok = all(pc[e] == len(acts[e]) for e in self.ENGS)
        return ok, stuck, semv

    def emit(self, nc, sems, lane_sems):
        acts = self.schedule()
        ok, stuck, semv = self.simulate(acts)
        if not ok:
            raise RuntimeError(f"sync deadlock in schedule: {stuck}")

        def semof(name):
            return lane_sems[name[1:]] if name[0] == "L" else sems[name[1:]]

        def body(ename):
            def run(e):
                for a in acts[ename]:
                    if a[0] == "wait":
                        e.wait_ge(semof(a[1]), a[2])
                    else:
                        ins = a[1].fn(e)
                        if a[2] is not None:
                            ins.then_inc(semof(a[2]), a[3])
            return run

        with nc.Block() as block:
            block.tensor(body("pe"))
            block.scalar(body("act"))
            block.vector(body("dve"))
            block.gpsimd(body("pool"))
            block.sync(body("sp"))


def build_nc(stage=99):
    nc = bass.Bass("TRN2", target_bir_lowering=False)
    P = Prog()

    def din(name, shape):
        return nc.dram_tensor(name, list(shape), F32, kind="ExternalInput").ap()

    def dout(name, shape):
        return nc.dram_tensor(name, list(shape), F32, kind="ExternalOutput").ap()

    xin = din("xin", [T, D])
    posf = din("posf", [128, 16])
    sa = din("sa", [2, NS * 30, DA])
    sb = din("sb", [2, NS * 2, DA])
    spl = din("spl", [2, NS * 15, D])
    norm_mix = din("norm_mix", [4, D])
    norm_mlp = din("norm_mlp", [4, D])
    norm_final = din("norm_final", [D])
    big = stage > 0
    w_in = din("w_in_even", [2, D, 5120] if big else [1, 1])
    conv_a_w = din("conv_a_w", [2, 31, DA])
    conv_a_b = din("conv_a_b", [2, DA])
    ln_a_g = din("ln_a_g", [2, DA])
    ln_a_b = din("ln_a_b", [2, DA])
    conv_b_w = din("conv_b_w", [2, 3, DA])
    w_out = din("w_out_even", [2, D, D] if big else [1, 1])
    pool_w = din("pool_w", [2, 4, 512, 512] if big else [1, 1])
    pool_scale = din("pool_scale", [2, D])
    w_up = din("w_mlp_up", [4, D, DFF] if big else [1, 1])
    w_down = din("w_mlp_down", [4, DFF, D] if big else [1, 1])

    yout = dout("yout", [T, D])
    nap = dout("nap", [2, 30, DA])
    nbp = dout("nbp", [2, 2, DA])
    npp = dout("npp", [2, 15, D])
    nas = dout("nas", [2, NS * 30, DA])
    nbs = dout("nbs", [2, NS * 2, DA])
    nps = dout("nps", [2, NS * 15, D])

    es = contextlib.ExitStack()
    with es:
        def sb_t(name, n, dt=F32):
            return es.enter_context(nc.sbuf_tensor(name, [128, n], dt))

        X = sb_t("X", KC * T)
        H = sb_t("H", KC * T, BF16)
        R1 = sb_t("R1", KC * T, BF16)
        WS = [sb_t(f"W{i}", 16 * 128, BF16) for i in range(NW)]
        RSTD = sb_t("RSTD", T)
        R2 = sb_t("R2", T)
        S1 = sb_t("S1", T)
        S2 = sb_t("S2", T)
        TA = sb_t("TA", 1408)
        TB = sb_t("TB", 1664)
        TC = sb_t("TC", 1408)
        CONST = sb_t("CONST", 768)
        IDENT = sb_t("IDENT", 128)
        ONES = sb_t("ONES", 128)
        FIX = sb_t("FIX", 64)
        POSF = sb_t("POSF", 16)
        DG = [sb_t(f"DG{i}", 128) for i in range(3)]
        STG = [sb_t(f"STG{i}", 512) for i in range(2)]
        GS = [sb_t(f"GS{i}", 64) for i in range(2)]
        STO = [sb_t(f"STO{i}", 256) for i in range(2)]
        PS = [es.enter_context(nc.psum_tensor(f"ps{i}", [128, 512], F32)) for i in range(8)]
        sems = {e: es.enter_context(nc.semaphore(f"sem_{e}")) for e in Prog.ENGS}
        lane_sems = {}
        for i in range(NW):
            lane_sems[f"w{i}"] = es.enter_context(nc.semaphore(f"lw{i}"))
        for i in range(NLANE):
            lane_sems[f"s{i}"] = es.enter_context(nc.semaphore(f"ls{i}"))

        R1f = R1[:, :].bitcast(F32)

        sp_lane = [0]

        def sp_dma(out, in_, reads=(), writes=()):
            ln = f"s{sp_lane[0] % NLANE}"
            sp_lane[0] += 1
            return P.add("sp", lambda e, o=out, i=in_: e.dma_start(out=o, in_=i), reads, writes, lane=ln)

        def Xc(c, lo=0, hi=T):
            return X[:, c * T + lo:c * T + hi]

        def Hc(c, lo=0, hi=T):
            return H[:, c * T + lo:c * T + hi]

        def kX(c, lo=0, hi=T):
            return ("X", c * T + lo, c * T + hi)

        def kH(c, lo=0, hi=T):
            return ("H", c * T + lo, c * T + hi)

        def kR1b(lo, hi):
            return ("R1", lo, hi)

        def act_op(out, in_, func, reads, writes, scale=None, bias=None):
            kw = {}
            if scale is not None:
                kw["scale"] = scale
            if bias is not None:
                kw["bias"] = bias
            return P.add("act", lambda e: e.activation(out=out, in_=in_, func=func, **kw), reads, writes)

        def tt_op(eng, out, in0, in1, op, reads, writes):
            return P.add(eng, lambda e: e.tensor_tensor(out=out, in0=in0, in1=in1, op=op), reads, writes)

        def ts_op(eng, out, in0, s1, s2, op0, op1, reads, writes):
            if s2 is None:
                s2, op1 = 0.0, ALU.add
            return P.add(eng, lambda e: e.tensor_scalar(out=out, in0=in0, scalar1=s1, scalar2=s2, op0=op0, op1=op1), reads, writes)

        def stt_op(eng, out, in0, scalar, in1, op0, op1, reads, writes):
            return P.add(eng, lambda e: e.scalar_tensor_tensor(out=out, in0=in0, scalar=scalar, in1=in1, op0=op0, op1=op1), reads, writes)

        def copy_op(eng, out, in_, reads, writes):
            if eng == "act":
                return P.add("act", lambda e: e.copy(out=out, in_=in_), reads, writes)
            return P.add(eng, lambda e: e.tensor_copy(out=out, in_=in_), reads, writes)

        def memset_op(eng, ap, val, writes):
            return P.add(eng, lambda e: e.memset(ap, val), (), writes)

        def mm_op(out, lhsT, rhs, start, stop, reads, writes):
            return P.add("pe", lambda e: e.matmul(out, lhsT, rhs, start=start, stop=stop), reads, writes)

        def tr_op(out, in_, reads, writes, n):
            return P.add("pe", lambda e: e.transpose(out, in_, IDENT[0:n, 0:n] if False else IDENT[:, :]), reads, writes)

        job_ctr = [0]

        def next_slot():
            s = job_ctr[0] % 2
            job_ctr[0] += 1
            return s

        def ps_tile(slot, tt):
            return PS[slot * 3 + tt][:, 0:TT]

        def kps(slot, tt):
            return (f"PS{slot * 3 + tt}", 0, TT)

        misc_ctr = [0]

        def next_misc():
            b = 6 + (misc_ctr[0] % 2)
            misc_ctr[0] += 1
            return b

        units = []

        def plan_units():
            for l in range(4):
                if l % 2 == 0:
                    e = l // 2
                    wv = w_in[e].rearrange("(k p) n -> p k n", p=128)
                    order = []
                    for c in range(8):
                        order += [8 + c, c]
                    for c in range(8):
                        order += [24 + c, 32 + c, 16 + c]
                    for blk in order:
                        units.append((wv[:, :, blk * 128:(blk + 1) * 128], 16))
                    wo = w_out[e].rearrange("(k p) n -> p k n", p=128)
                    for m in range(16):
                        units.append((wo[:, :, m * 128:(m + 1) * 128], 16))
                else:
                    o = l // 2
                    for g in range(4):
                        pw = pool_w[o, g].rearrange("(k p) n -> p k n", p=128)
                        for m in range(4):
                            units.append((pw[:, :, m * 128:(m + 1) * 128], 4))
                wu = w_up[l].rearrange("(k p) n -> p k n", p=128)
                for gi in range(4):
                    for fj in range(16):
                        f = gi * 16 + fj
                        units.append((wu[:, :, f * 128:(f + 1) * 128], 16))
                    wd = w_down[l][gi * 2048:(gi + 1) * 2048, :].rearrange("(k p) n -> p k n", p=128)
                    for m in range(16):
                        units.append((wd[:, :, m * 128:(m + 1) * 128], 16))

        if big:
            plan_units()
        unit_issued = [0]
        unit_next = [0]

        def issue_unit(u):
            if u >= len(units):
                return
            src, nk = units[u]
            slot = u % NW
            dst = WS[slot][:, 0:nk * 128].rearrange("p (k n) -> p k n", n=128)
            P.add("pool", lambda e, o=dst, i=src: e.dma_start(out=o, in_=i), (), [(f"W{slot}", 0, nk * 128)], lane=f"w{slot}")

        def take_unit():
            u = unit_next[0]
            unit_next[0] += 1
            while unit_issued[0] < min(u + NW, len(units)):
                issue_unit(unit_issued[0])
                unit_issued[0] += 1
            slot = u % NW
            nk = units[u][1]
            return slot, nk

        def mm_job(rhs_fn, rhs_key_fn):
            wslot, nk = take_unit()
            slot = next_slot()
            W3 = WS[wslot][:, 0:nk * 128].rearrange("p (k n) -> p k n", n=128)
            for k in range(nk):
                for tt in range(NTT):
                    mm_op(ps_tile(slot, tt), W3[:, k, :], rhs_fn(k, tt), k == 0, k == nk - 1,
                          [(f"W{wslot}", k * 128, (k + 1) * 128), rhs_key_fn(k, tt)], [kps(slot, tt)])
            return slot

        cst_src = [
            ("gmix", norm_mix.rearrange("l (c p) -> (l c) p", p=128), 64),
            ("gmlp", norm_mlp.rearrange("l (c p) -> (l c) p", p=128), 64),
            ("gfin", norm_final.rearrange("(c p) -> c p", p=128), 16),
            ("caw", conv_a_w.rearrange("e k (c p) -> (e k c) p", p=128), 496),
            ("cab", conv_a_b.rearrange("e (c p) -> (e c) p", p=128), 16),
            ("lng", ln_a_g.rearrange("e (c p) -> (e c) p", p=128), 16),
            ("lnb", ln_a_b.rearrange("e (c p) -> (e c) p", p=128), 16),
            ("cbw", conv_b_w.rearrange("e k (c p) -> (e k c) p", p=128), 48),
            ("psc", pool_scale.rearrange("o (c p) -> (o c) p", p=128), 32),
        ]
        coff = {}
        row = 0
        for name, ap, n in cst_src:
            coff[name] = row
            r = 0
            while r < n:
                blk = (row + r) // 128
                p0 = (row + r) % 128
                cnt = min(n - r, 128 - p0)
                sp_dma(R1f[p0:p0 + cnt, blk * 128:(blk + 1) * 128], ap[r:r + cnt, :], (),
                       [("R1", blk * 512, (blk + 1) * 512)])
                r += cnt
            row += n
        assert row == 768
        memset_op("pool", ONES[:, :], 1.0, [("ONES", 0, 128)])
        P.add("pool", lambda e: e.affine_select(out=IDENT[:, :], in_=ONES[:, :], pattern=[[1, 128]],
                                                compare_op=ALU.is_equal, fill=0.0, base=0, channel_multiplier=-1),
              [("ONES", 0, 128)], [("IDENT", 0, 128)])
        for blk in range(6):
            b = next_misc()
            P.add("pe", lambda e, b=b, blk=blk: e.transpose(PS[b][:, 0:128], R1f[:, blk * 128:(blk + 1) * 128], IDENT[:, :]),
                  [("R1", blk * 512, (blk + 1) * 512), ("IDENT", 0, 128)], [(f"PS{b}", 0, 128)])
            copy_op("dve", CONST[:, blk * 128:(blk + 1) * 128], PS[b][:, 0:128], [(f"PS{b}", 0, 128)],
                    [("CONST", blk * 128, (blk + 1) * 128)])

        def ccol(name, idx):
            o = coff[name] + idx
            return CONST[:, o:o + 1]

        def kc():
            return ("CONST", 0, 768)

        sp_dma(POSF[:, :], posf[:, :], (), [("POSF", 0, 16)])
        for g in range(4):
            w = float(2 ** (g + 1))
            fx = FIX[:, g * 16:(g + 1) * 16]
            ts_op("dve", fx, POSF[:, :], 1.0, w, ALU.add, ALU.min, [("POSF", 0, 16)], [("FIX", g * 16, g * 16 + 16)])
            P.add("dve", lambda e, fx=fx: e.reciprocal(out=fx, in_=fx), [("FIX", g * 16, g * 16 + 16)], [("FIX", g * 16, g * 16 + 16)])
            ts_op("dve", fx, fx, w, None, ALU.mult, None, [("FIX", g * 16, g * 16 + 16)], [("FIX", g * 16, g * 16 + 16)])

        def stage4(i):
            return R1f[:, i * 2048:(i + 1) * 2048]

        def kstage4(i, lo=0, hi=2048):
            return ("R1", i * 8192 + lo * 4, i * 8192 + hi * 4)

        cp_alt = [0]

        def alt_eng():
            cp_alt[0] += 1
            return "act" if cp_alt[0] % 2 else "dve"

        for blk in range(T // 128):
            si = blk % 4
            sp_dma(stage4(si), xin[blk * 128:(blk + 1) * 128, :], (), [kstage4(si)])
            for cg in range(4):
                b = (blk * 4 + cg) % 8
                for j in range(4):
                    c = cg * 4 + j
                    P.add("pe", lambda e, b=b, j=j, c=c, si=si: e.transpose(PS[b][:, j * 128:(j + 1) * 128], stage4(si)[:, c * 128:(c + 1) * 128], IDENT[:, :]),
                          [kstage4(si, c * 128, (c + 1) * 128), ("IDENT", 0, 128)], [(f"PS{b}", j * 128, (j + 1) * 128)])
                outv = X[:, :].rearrange("p (c t) -> p c t", t=T)[:, cg * 4:(cg + 1) * 4, blk * 128:(blk + 1) * 128]
                inv = PS[b][:, :].rearrange("p (c t) -> p c t", t=128)
                copy_op(alt_eng(), outv, inv, [(f"PS{b}", 0, 512)], [kX(c, blk * 128, (blk + 1) * 128) for c in range(cg * 4, cg * 4 + 4)])

        DBG = os.environ.get("K_DBG", "")
        for e_ in range(0 if "nod2d" in DBG else 2):
            sp_dma(nas[e_].rearrange("(s r) c -> s r c", r=30)[:, 0:26, :],
                   sa[e_].rearrange("(s r) c -> s r c", r=30)[:, 4:30, :])
            sp_dma(nps[e_].rearrange("(s r) c -> s r c", r=15)[:, 0:11, :],
                   spl[e_].rearrange("(s r) c -> s r c", r=15)[:, 4:15, :])

        def nchunk(spec, c, add_eng="dve"):
            gname, gidx, make_h = spec
            if c == 0:
                act_op(S2[:, :], Xc(c), AF.Square, [kX(c)], [("S2", 0, T)])
            else:
                tmp, tk = (S1, "S1") if c % 2 else (R2, "R2")
                act_op(tmp[:, 0:T], Xc(c), AF.Square, [kX(c)], [(tk, 0, T)])
                tt_op(add_eng, S2[:, :], S2[:, :], tmp[:, 0:T], ALU.add, [("S2", 0, T), (tk, 0, T)], [("S2", 0, T)])
            if make_h:
                act_op(Hc(c), Xc(c), AF.Copy, [kX(c), kc()], [kH(c)], scale=ccol(gname, gidx * 16 + c))

        def nfinish():
            slot = next_slot()
            for tt in range(NTT):
                mm_op(ps_tile(slot, tt), ONES[:, :], S2[:, tt * TT:(tt + 1) * TT], True, True,
                      [("ONES", 0, 128), ("S2", tt * TT, (tt + 1) * TT)], [kps(slot, tt)])
                rs = RSTD[:, tt * TT:(tt + 1) * TT]
                ts_op("dve", rs, ps_tile(slot, tt), 1.0 / D, EPS, ALU.mult, ALU.add, [kps(slot, tt)], [("RSTD", tt * TT, (tt + 1) * TT)])
                act_op(rs, rs, AF.Sqrt, [("RSTD", tt * TT, (tt + 1) * TT)], [("RSTD", tt * TT, (tt + 1) * TT)])
                P.add("dve", lambda e, rs=rs: e.reciprocal(out=rs, in_=rs), [("RSTD", tt * TT, (tt + 1) * TT)], [("RSTD", tt * TT, (tt + 1) * TT)])

        def rms_prep(spec):
            for c in range(KC):
                nchunk(spec, c)
            nfinish()

        def mix_spec(l):
            if l >= nlayers:
                return ("gfin", 0, False)
            return ("gmix", l, l % 2 == 0)

        def mlp(l):
            nxt = mix_spec(l + 1)
            for gi in range(4):
                for fj in range(16):
                    slot = mm_job(lambda k, tt: Hc(k, tt * TT, (tt + 1) * TT), lambda k, tt: kH(k, tt * TT, (tt + 1) * TT))
                    tmp, tk = (TA, "TA") if fj % 2 else (TC, "TC")
                    for tt in range(NTT):
                        tv = tmp[:, tt * TT:(tt + 1) * TT]
                        stt_op("dve", tv, ps_tile(slot, tt), 0.0, RSTD[:, tt * TT:(tt + 1) * TT], ALU.max, ALU.mult,
                               [kps(slot, tt), ("RSTD", tt * TT, (tt + 1) * TT)], [(tk, tt * TT, (tt + 1) * TT)])
                        lo = fj * T + tt * TT
                        act_op(R1[:, lo:lo + TT], tv, AF.Square, [(tk, tt * TT, (tt + 1) * TT)], [kR1b(lo * 2, (lo + TT) * 2)])
                for m in range(16):
                    slot = mm_job(lambda k, tt: R1[:, k * T + tt * TT:k * T + (tt + 1) * TT],
                                  lambda k, tt: kR1b((k * T + tt * TT) * 2, (k * T + (tt + 1) * TT) * 2))
                    for tt in range(NTT):
                        xv = Xc(m, tt * TT, (tt + 1) * TT)
                        tt_op("dve", xv, ps_tile(slot, tt), xv, ALU.add, [kps(slot, tt), kX(m, tt * TT, (tt + 1) * TT)],
                              [kX(m, tt * TT, (tt + 1) * TT)])
                    if gi == 3:
                        nchunk(nxt, m)
            nfinish()

        sto_ctr = [0]

        def state_out_prep(in_s, nt, key_in):
            i = sto_ctr[0] % 2
            sto_ctr[0] += 1
            gs = GS[i]
            copy_op("dve", gs[:, 0:nt * 16].rearrange("p (t s) -> p t s", s=16), in_s, [key_in], [(f"GS{i}", 0, 64)])
            return i

        def state_out_fin(i, in_p, np_, nt, key_in, dst_p, dst_s_fn):
            b = next_misc()
            sto = STO[i]
            gs = GS[i]
            P.add("pe", lambda e: e.transpose(PS[b][0:np_, 0:128], in_p, IDENT[:, :]), [key_in, ("IDENT", 0, 128)], [(f"PS{b}", 0, 256)])
            P.add("pe", lambda e: e.transpose(PS[b][0:nt * 16, 128:256], gs[:, 0:nt * 16], IDENT[:, :]), [(f"GS{i}", 0, 64), ("IDENT", 0, 128)], [(f"PS{b}", 0, 256)])
            copy_op("act", sto[0:np_, 0:128], PS[b][0:np_, 0:128], [(f"PS{b}", 0, 128)], [(f"STO{i}", 0, 128)])
            copy_op("act", sto[0:nt * 16, 128:256], PS[b][0:nt * 16, 128:256], [(f"PS{b}", 128, 256)], [(f"STO{i}", 128, 256)])
            sp_dma(dst_p, sto[0:np_, 0:128], [(f"STO{i}", 0, 128)], ())
            for t in range(nt):
                sp_dma(dst_s_fn(t), sto[t * 16:(t + 1) * 16, 128:256], [(f"STO{i}", 128, 256)], ())

        def state_out(in_p, np_, in_s, nt, key_in, dst_p, dst_s_fn):
            i = state_out_prep(in_s, nt, key_in)
            state_out_fin(i, in_p, np_, nt, key_in, dst_p, dst_s_fn)

        stg_ctr = [0]

        def state_load(src_rows, nblk, rows_per_blk):
            i = stg_ctr[0] % 2
            stg_ctr[0] += 1
            stg = STG[i]
            for q in range(nblk):
                sp_dma(stg[0:rows_per_blk, q * 128:(q + 1) * 128], src_rows[q * rows_per_blk:(q + 1) * rows_per_blk, :], (),
                       [(f"STG{i}", q * 128, (q + 1) * 128)])
            return i

        def state_xpose(i, nblk, rows_per_blk, dst3, dkeys, hist):
            stg = STG[i]
            b = next_misc()
            for q in range(nblk):
                P.add("pe", lambda e, q=q: e.transpose(PS[b][:, q * rows_per_blk:(q + 1) * rows_per_blk], stg[0:rows_per_blk, q * 128:(q + 1) * 128], IDENT[0:rows_per_blk, 0:rows_per_blk]),
                      [(f"STG{i}", q * 128, (q + 1) * 128), ("IDENT", 0, 128)], [(f"PS{b}", q * rows_per_blk, (q + 1) * rows_per_blk)])
            n = nblk * rows_per_blk
            copy_op("act", dst3, PS[b][:, 0:n].rearrange("p (s r) -> p s r", r=hist), [(f"PS{b}", 0, n)], dkeys)

        def state_in(src_rows, nblk, rows_per_blk, dst3, dkeys, hist):
            i = state_load(src_rows, nblk, rows_per_blk)
            state_xpose(i, nblk, rows_per_blk, dst3, dkeys, hist)

        def even_mixer(l):
            e_ = l // 2
            AOFF = 30
            ASO = 30 + PP
            TBs = TB[:, ASO:ASO + NS * 34].rearrange("p (s j) -> p s j", j=34)
            kTB = ("TB", 0, 1664)
            memset_op("pool", TB[:, 0:30], 0.0, [("TB", 0, 30)])
            sav = sa[e_]
            Hrhs = (lambda k, tt: Hc(k, tt * TT, (tt + 1) * TT), lambda k, tt: kH(k, tt * TT, (tt + 1) * TT))

            def a_jobs_gate():
                s_gate = mm_job(*Hrhs)
                for tt in range(NTT):
                    copy_op("act", TA[:, tt * TT:(tt + 1) * TT], ps_tile(s_gate, tt), [kps(s_gate, tt)], [("TA", tt * TT, (tt + 1) * TT)])

            def a_gate_sig():
                tt_op("dve", TA[:, 0:T], TA[:, 0:T], RSTD[:, :], ALU.mult, [("TA", 0, T), ("RSTD", 0, T)], [("TA", 0, T)])
                act_op(TA[:, 0:T], TA[:, 0:T], AF.Sigmoid, [("TA", 0, T)], [("TA", 0, T)])

            NPE = int(os.environ.get("K_NPE", "10"))
            dg_ctr = [0]

            def conv_pe(c):
                slot = next_slot()
                taps = list(range(31 - NPE, 31))
                for j, k in enumerate(taps):
                    di = dg_ctr[0] % 3
                    dg_ctr[0] += 1
                    wk = ccol("caw", (e_ * 31 + k) * 8 + c)
                    act_op(DG[di][:, :], IDENT[:, :], AF.Copy, [("IDENT", 0, 128), kc()], [(f"DG{di}", 0, 128)], scale=wk)
                    for tt in range(NTT):
                        ncol = TT if tt < 2 else PP - 2 * TT
                        mm_op(PS[slot * 3 + tt][:, 0:ncol], DG[di][:, :], TB[:, k + tt * TT:k + tt * TT + ncol], j == 0, j == len(taps) - 1,
                              [(f"DG{di}", 0, 128), kTB], [kps(slot, tt)])
                return slot

            ld = state_load(sav[:, 0:128], 4, 120)
            a_jobs_gate()
            a_gate_sig()
            s_val = mm_job(*Hrhs)
            for c in range(8):
                ld_next = state_load(sav[:, (c + 1) * 128:(c + 2) * 128], 4, 120) if c + 1 < 8 else None
                state_xpose(ld, 4, 120, TBs[:, :, 0:30], [("TB", ASO, ASO + NS * 34)], 30)
                ld = ld_next
                tt_op("dve", TA[:, 0:T], TA[:, 0:T], RSTD[:, :], ALU.mult, [("TA", 0, T), ("RSTD", 0, T)], [("TA", 0, T)])
                for tt in range(2):
                    tt_op("dve", TB[:, AOFF + tt * TT:AOFF + (tt + 1) * TT], ps_tile(s_val, tt), TA[:, tt * TT:(tt + 1) * TT], ALU.mult,
                          [kps(s_val, tt), ("TA", tt * TT, (tt + 1) * TT)], [("TB", AOFF + tt * TT, AOFF + (tt + 1) * TT)])
                np2 = PP - 2 * TT
                tt_op("dve", TB[:, AOFF + 2 * TT:AOFF + PP], PS[s_val * 3 + 2][:, 0:np2], TA[:, 2 * TT:PP], ALU.mult,
                      [kps(s_val, 2), ("TA", 2 * TT, PP)], [("TB", AOFF + 2 * TT, AOFF + PP)])
                tt_op("dve", TBs[:, :, 30:34], PS[s_val * 3 + 2][:, np2:np2 + 64].rearrange("p (s j) -> p s j", j=4),
                      TA[:, S0:SE].rearrange("p (s j) -> p s j", j=4), ALU.mult,
                      [kps(s_val, 2), ("TA", S0, SE)], [("TB", ASO, ASO + NS * 34)])
                state_out(TB[:, AOFF + PP - 30:AOFF + PP], 30, TBs[:, :, 30:34].rearrange("p s t -> p t s"), 4, kTB,
                          nap[e_, :, c * 128:(c + 1) * 128],
                          lambda t, c=c: nas[e_].rearrange("(s r) n -> s r n", r=30)[:, 26 + t, c * 128:(c + 1) * 128])
                if c + 1 < 8:
                    a_jobs_gate()
                    s_val_next = mm_job(*Hrhs)
                s_cv = conv_pe(c) if NPE > 0 else None
                acc = R1f[:, c * T:(c + 1) * T]
                kacc = kR1b(c * T * 4, (c + 1) * T * 4)
                accs = acc[:, S0:SE].rearrange("p (s j) -> p s j", j=4)
                memset_op("pool", acc[:, SE:T], 0.0, [kR1b((c * T + SE) * 4, (c + 1) * T * 4)])
                for k in range(31):
                    wk = ccol("caw", (e_ * 31 + k) * 8 + c)
                    if k == 0:
                        ts_op("dve", acc[:, 0:PP], TB[:, 0:PP], wk, ccol("cab", e_ * 8 + c), ALU.mult, ALU.add,
                              [kTB, kc()], [kR1b(c * T * 4, (c * T + PP) * 4)])
                        ts_op("dve", accs, TBs[:, :, 0:4], wk, ccol("cab", e_ * 8 + c), ALU.mult, ALU.add,
                              [kTB, kc()], [kR1b((c * T + S0) * 4, (c * T + SE) * 4)])
                    else:
                        if k < 31 - NPE:
                            stt_op("dve", acc[:, 0:PP], TB[:, k:k + PP], wk, acc[:, 0:PP], ALU.mult, ALU.add,
                                   [kTB, kc(), kR1b(c * T * 4, (c * T + PP) * 4)], [kR1b(c * T * 4, (c * T + PP) * 4)])
                        stt_op("dve", accs, TBs[:, :, k:k + 4], wk, accs, ALU.mult, ALU.add,
                               [kTB, kc(), kR1b((c * T + S0) * 4, (c * T + SE) * 4)], [kR1b((c * T + S0) * 4, (c * T + SE) * 4)])
                if c + 1 < 8:
                    a_gate_sig()
                if s_cv is not None:
                    for tt in range(NTT):
                        ncol = TT if tt < 2 else PP - 2 * TT
                        av = acc[:, tt * TT:tt * TT + ncol]
                        ka = kR1b((c * T + tt * TT) * 4, (c * T + tt * TT + ncol) * 4)
                        tt_op("dve", av, PS[s_cv * 3 + tt][:, 0:ncol], av, ALU.add, [kps(s_cv, tt), ka], [ka])
                if c == 0:
                    act_op(S2[:, :], acc, AF.Square, [kacc], [("S2", 0, T)])
                    copy_op("dve", S1[:, :], acc, [kacc], [("S1", 0, T)])
                else:
                    act_op(TC[:, 0:T], acc, AF.Square, [kacc], [("TC", 0, T)])
                    tt_op("dve", S1[:, :], S1[:, :], acc, ALU.add, [("S1", 0, T), kacc], [("S1", 0, T)])
                    tt_op("dve", S2[:, :], S2[:, :], TC[:, 0:T], ALU.add, [("S2", 0, T), ("TC", 0, T)], [("S2", 0, T)])
                if c + 1 < 8:
                    s_val = s_val_next
            s_1 = next_slot()
            for tt in range(NTT):
                mm_op(ps_tile(s_1, tt), ONES[:, :], S1[:, tt * TT:(tt + 1) * TT], True, True,
                      [("ONES", 0, 128), ("S1", tt * TT, (tt + 1) * TT)], [kps(s_1, tt)])
            s_2 = next_slot()
            for tt in range(NTT):
                mm_op(ps_tile(s_2, tt), ONES[:, :], S2[:, tt * TT:(tt + 1) * TT], True, True,
                      [("ONES", 0, 128), ("S2", tt * TT, (tt + 1) * TT)], [kps(s_2, tt)])
            for tt in range(NTT):
                sl = slice(tt * TT, (tt + 1) * TT)
                kM = ("R2", tt * TT, (tt + 1) * TT)
                kV = ("S2", tt * TT, (tt + 1) * TT)
                ts_op("dve", R2[:, sl], ps_tile(s_1, tt), 1.0 / DA, None, ALU.mult, None, [kps(s_1, tt)], [kM])
                tt_op("dve", S2[:, sl], R2[:, sl], R2[:, sl], ALU.mult, [kM], [kV])
                stt_op("dve", S2[:, sl], ps_tile(s_2, tt), 1.0 / DA, S2[:, sl], ALU.mult, ALU.subtract, [kps(s_2, tt), kV], [kV])
                ts_op("dve", S2[:, sl], S2[:, sl], 0.0, EPS, ALU.max, ALU.add, [kV], [kV])
                act_op(S2[:, sl], S2[:, sl], AF.Sqrt, [kV], [kV])
                P.add("dve", lambda e, v=S2[:, sl]: e.reciprocal(out=v, in_=v), [kV], [kV])
                stt_op("dve", R2[:, sl], R2[:, sl], -1.0, S2[:, sl], ALU.mult, ALU.mult, [kM, kV], [kM])

            def ln_apply(c):
                acc = R1f[:, c * T:(c + 1) * T]
                kacc = kR1b(c * T * 4, (c + 1) * T * 4)
                tt_op("dve", acc, acc, S2[:, 0:T], ALU.mult, [kacc, ("S2", 0, T)], [kacc])
                tt_op("dve", acc, acc, R2[:, 0:T], ALU.add, [kacc, ("R2", 0, T)], [kacc])
                act_op(R1[:, c * 2 * T:c * 2 * T + T], acc, AF.Silu, [kacc, kc()], [kacc],
                       scale=ccol("lng", e_ * 8 + c), bias=ccol("lnb", e_ * 8 + c))
            VOFF = 2
            VSO = 2 + PP
            TBv = TB[:, VSO:VSO + NS * 6].rearrange("p (s j) -> p s j", j=6)
            memset_op("pool", TB[:, 0:2], 0.0, [("TB", 0, 2)])
            memset_op("pool", S1[:, SE:T], 0.0, [("S1", SE, T)])
            sbv = sb[e_]
            ldb = state_load(sbv[:, 0:128], 1, 32)
            for c in range(8):
                ldb_next = state_load(sbv[:, (c + 1) * 128:(c + 2) * 128], 1, 32) if c + 1 < 8 else None
                state_xpose(ldb, 1, 32, TBv[:, :, 0:2], [("TB", VSO, VSO + NS * 6)], 2)
                ldb = ldb_next
                s_cg = mm_job(lambda k, tt: Hc(k, tt * TT, (tt + 1) * TT), lambda k, tt: kH(k, tt * TT, (tt + 1) * TT))
                ln_apply(c)
                for tt in range(NTT):
                    tt_op("dve", TA[:, tt * TT:(tt + 1) * TT], ps_tile(s_cg, tt), RSTD[:, tt * TT:(tt + 1) * TT], ALU.mult,
                          [kps(s_cg, tt), ("RSTD", tt * TT, (tt + 1) * TT)], [("TA", tt * TT, (tt + 1) * TT)])
                tt_op("dve", TA[:, 0:T], TA[:, 0:T], RSTD[:, :], ALU.mult, [("TA", 0, T), ("RSTD", 0, T)], [("TA", 0, T)])
                s_bx = mm_job(lambda k, tt: Hc(k, tt * TT, (tt + 1) * TT), lambda k, tt: kH(k, tt * TT, (tt + 1) * TT))
                for tt in range(2):
                    tt_op("dve", TB[:, VOFF + tt * TT:VOFF + (tt + 1) * TT], ps_tile(s_bx, tt), TA[:, tt * TT:(tt + 1) * TT], ALU.mult,
                          [kps(s_bx, tt), ("TA", tt * TT, (tt + 1) * TT)], [("TB", VOFF + tt * TT, VOFF + (tt + 1) * TT)])
                np2 = PP - 2 * TT
                tt_op("dve", TB[:, VOFF + 2 * TT:VOFF + PP], PS[s_bx * 3 + 2][:, 0:np2], TA[:, 2 * TT:PP], ALU.mult,
                      [kps(s_bx, 2), ("TA", 2 * TT, PP)], [("TB", VOFF + 2 * TT, VOFF + PP)])
                tt_op("dve", TBv[:, :, 2:6], PS[s_bx * 3 + 2][:, np2:np2 + 64].rearrange("p (s j) -> p s j", j=4),
                      TA[:, S0:SE].rearrange("p (s j) -> p s j", j=4), ALU.mult,
                      [kps(s_bx, 2), ("TA", S0, SE)], [("TB", VSO, VSO + NS * 6)])
                so_i = state_out_prep(TBv[:, :, 4:6].rearrange("p s t -> p t s"), 2, kTB)
                S1s = S1[:, S0:SE].rearrange("p (s j) -> p s j", j=4)
                for k in range(3):
                    wk = ccol("cbw", (e_ * 3 + k) * 8 + c)
                    if k == 0:
                        ts_op("dve", S1[:, 0:PP], TB[:, 0:PP], wk, None, ALU.mult, None, [kTB, kc()], [("S1", 0, PP)])
                        ts_op("dve", S1s, TBv[:, :, 0:4], wk, None, ALU.mult, None, [kTB, kc()], [("S1", S0, SE)])
                    else:
                        stt_op("dve", S1[:, 0:PP], TB[:, k:k + PP], wk, S1[:, 0:PP], ALU.mult, ALU.add, [kTB, kc(), ("S1", 0, PP)], [("S1", 0, PP)])
                        stt_op("dve", S1s, TBv[:, :, k:k + 4], wk, S1s, ALU.mult, ALU.add, [kTB, kc(), ("S1", S0, SE)], [("S1", S0, SE)])
                s_bg = mm_job(lambda k, tt: Hc(k, tt * TT, (tt + 1) * TT), lambda k, tt: kH(k, tt * TT, (tt + 1) * TT))
                for tt in range(NTT):
                    tt_op("dve", TC[:, tt * TT:(tt + 1) * TT], ps_tile(s_bg, tt), RSTD[:, tt * TT:(tt + 1) * TT], ALU.mult,
                          [kps(s_bg, tt), ("RSTD", tt * TT, (tt + 1) * TT)], [("TC", tt * TT, (tt + 1) * TT)])
                bo = c * 2 * T + T
                tt_op("dve", R1[:, bo:bo + T], TC[:, 0:T], S1[:, :], ALU.mult, [("TC", 0, T), ("S1", 0, T)], [kR1b(bo * 2, (bo + T) * 2)])
                state_out_fin(so_i, TB[:, VOFF + PP - 2:VOFF + PP], 2, 2, kTB,
                          nbp[e_, :, c * 128:(c + 1) * 128],
                          lambda t, c=c: nbs[e_].rearrange("(s r) n -> s r n", r=2)[:, t, c * 128:(c + 1) * 128])
            for m in range(16):
                def rhs(k, tt):
                    base = (k % 8) * 2 * T + (T if k >= 8 else 0)
                    return R1[:, base + tt * TT:base + (tt + 1) * TT]

                def rkey(k, tt):
                    base = (k % 8) * 2 * T + (T if k >= 8 else 0)
                    return kR1b((base + tt * TT) * 2, (base + (tt + 1) * TT) * 2)
                slot = mm_job(rhs, rkey)
                for tt in range(NTT):
                    xv = Xc(m, tt * TT, (tt + 1) * TT)
                    tt_op("dve", xv, ps_tile(slot, tt), xv, ALU.add, [kps(slot, tt), kX(m, tt * TT, (tt + 1) * TT)], [kX(m, tt * TT, (tt + 1) * TT)])
                nchunk(("gmlp", l, True), m)
            nfinish()

        def odd_mixer(l):
            o_ = l // 2
            EO = 15
            ESO = 15 + PP
            NE = ESO
            bufs = {"TA": TA, "TB": TB, "TC": TC}

            def sview(buf):
                return buf[:, ESO:ESO + NS * 19].rearrange("p (s j) -> p s j", j=19)
            for nm in ("TA", "TB", "TC"):
                memset_op("pool", bufs[nm][:, 0:15], 0.0, [(nm, 0, 15)])
            spv = spl[o_]
            ld = state_load(spv[:, 0:128], 2, 120)
            for c in range(KC):
                g = c // 4
                w = 2 ** (g + 1)
                E, Es = TA, sview(TA)
                ld_next = state_load(spv[:, (c + 1) * 128:(c + 2) * 128], 2, 120) if c + 1 < KC else None
                state_xpose(ld, 2, 120, Es[:, :, 0:15], [("TA", ESO, ESO + NS * 19)], 15)
                ld = ld_next
                gcol = ccol("gmix", l * 16 + c)
                stt_op("dve", E[:, EO:EO + PP], Xc(c, 0, PP), gcol, RSTD[:, 0:PP], ALU.mult, ALU.mult,
                       [kX(c, 0, PP), kc(), ("RSTD", 0, PP)], [("TA", EO, EO + PP)])
                stt_op("dve", Es[:, :, 15:19], Xc(c, S0, SE).rearrange("p (s j) -> p s j", j=4), gcol,
                       RSTD[:, S0:SE].rearrange("p (s j) -> p s j", j=4), ALU.mult, ALU.mult,
                       [kX(c, S0, SE), kc(), ("RSTD", S0, SE)], [("TA", ESO, ESO + NS * 19)])
                src, srcn = E, "TA"
                pp = [("TB", TB), ("TC", TC)]
                sh = 1
                step = 0
                while sh < w:
                    dn, dst = pp[step % 2]
                    lo = 2 * sh - 1
                    tt_op("dve", dst[:, lo:NE], src[:, lo:NE], src[:, lo - sh:NE - sh], ALU.add,
                          [(srcn, 0, NE)], [(dn, lo, NE)])
                    tt_op("dve", sview(dst)[:, :, lo:19], sview(src)[:, :, lo:19], sview(src)[:, :, lo - sh:19 - sh], ALU.add,
                          [(srcn, ESO, ESO + NS * 19)], [(dn, ESO, ESO + NS * 19)])
                    src, srcn = dst, dn
                    sh *= 2
                    step += 1
                Fb, Fn = src, srcn
                tt_op("dve", Fb[:, EO:EO + 16], Fb[:, EO:EO + 16], FIX[:, g * 16:(g + 1) * 16], ALU.mult,
                      [(Fn, EO, EO + 16), ("FIX", 0, 64)], [(Fn, EO, EO + 16)])
                stt_op("dve", Hc(c, 0, PP), Fb[:, EO:EO + PP], 1.0 / w, E[:, EO:EO + PP], ALU.mult, ALU.subtract,
                       [(Fn, EO, EO + PP), ("TA", EO, EO + PP)], [kH(c, 0, PP)])
                stt_op("dve", Hc(c, S0, SE).rearrange("p (s j) -> p s j", j=4), sview(Fb)[:, :, 15:19], 1.0 / w, Es[:, :, 15:19],
                       ALU.mult, ALU.subtract, [(Fn, ESO, ESO + NS * 19), ("TA", ESO, ESO + NS * 19)], [kH(c, S0, SE)])
                memset_op("pool", Hc(c, SE, T), 0.0, [kH(c, SE, T)])
                state_out(E[:, EO + PP - 15:EO + PP], 15, Es[:, :, 15:19].rearrange("p s t -> p t s"), 4, ("TA", 0, 1408),
                          npp[o_, :, c * 128:(c + 1) * 128],
                          lambda t, c=c: nps[o_].rearrange("(s r) n -> s r n", r=15)[:, 11 + t, c * 128:(c + 1) * 128])
                if c % 4 == 3:
                    for m in range(4):
                        slot = mm_job(lambda k, tt, g=g: Hc(g * 4 + k, tt * TT, (tt + 1) * TT),
                                      lambda k, tt, g=g: kH(g * 4 + k, tt * TT, (tt + 1) * TT))
                        xc = g * 4 + m
                        for tt in range(NTT):
                            xv = Xc(xc, tt * TT, (tt + 1) * TT)
                            stt_op("dve", xv, ps_tile(slot, tt), ccol("psc", o_ * 16 + xc), xv, ALU.mult, ALU.add,
                                   [kps(slot, tt), kc(), kX(xc, tt * TT, (tt + 1) * TT)], [kX(xc, tt * TT, (tt + 1) * TT)])
                    for m in range(4):
                        nchunk(("gmlp", l, True), g * 4 + m)
            nfinish()

        nlayers = min(stage, 4)
        rms_prep(mix_spec(0))
        for l in range(nlayers):
            if l % 2 == 0:
                even_mixer(l)
            else:
                odd_mixer(l)
            mlp(l)

        if "norms" in DBG:
            pass
        for c in range(0 if "norms" in DBG else KC):
            stt_op("dve", Xc(c), Xc(c), ccol("gfin", c), RSTD[:, :], ALU.mult, ALU.mult, [kX(c), kc(), ("RSTD", 0, T)], [kX(c)])
        for blk in range(T // 128):
            si = blk % 4
            for cg in range(4):
                b = (blk * 4 + cg) % 8
                for j in range(4):
                    c = cg * 4 + j
                    P.add("pe", lambda e, b=b, j=j, c=c, blk=blk: e.transpose(PS[b][:, j * 128:(j + 1) * 128], Xc(c, blk * 128, (blk + 1) * 128), IDENT[:, :]),
                          [kX(c, blk * 128, (blk + 1) * 128), ("IDENT", 0, 128)], [(f"PS{b}", j * 128, (j + 1) * 128)])
                copy_op(alt_eng(), stage4(si)[:, cg * 512:(cg + 1) * 512], PS[b][:, :], [(f"PS{b}", 0, 512)], [kstage4(si, cg * 512, (cg + 1) * 512)])
            sp_dma(yout[blk * 128:(blk + 1) * 128, :], stage4(si), [kstage4(si)], ())

        P.finalize()
        P.emit(nc, sems, lane_sems)
    return nc


_W_NAMES = ["norm_mix", "norm_mlp", "norm_final", "w_in_even", "conv_a_w", "conv_a_b", "ln_a_g", "ln_a_b",
            "conv_b_w", "w_out_even", "pool_w", "pool_scale", "w_mlp_up", "w_mlp_down"]


def make_in_maps(inputs):
    xp = np.asarray(inputs["x_prompt"], np.float32)
    xs = np.asarray(inputs["x_sample"], np.float32)
    meta = np.asarray(inputs["meta_tokens"], np.float32)
    sa = np.asarray(inputs["state_conv_a"], np.float32)
    sb = np.asarray(inputs["state_conv_b"], np.float32)
    spl = np.asarray(inputs["state_pool"], np.float32)
    wts = {k: np.ascontiguousarray(np.asarray(inputs[k], np.float32)) for k in _W_NAMES}
    if int(os.environ.get("K_STAGE", "99")) == 0:
        for k in ("w_in_even", "w_out_even", "pool_w", "w_mlp_up", "w_mlp_down"):
            wts[k] = np.zeros((1, 1), np.float32)
    in_maps = []
    for c in range(8):
        b, half = c // 2, c % 2
        xext = np.concatenate([meta, xp[b]], axis=0)
        start = 0 if half == 0 else 2064 - PP
        xin = np.zeros((T, D), np.float32)
        xin[0:PP] = xext[start:start + PP]
        xin[S0:SE] = xs[c * NS:(c + 1) * NS].reshape(NS * LS, D)
        posf = np.broadcast_to((start + np.arange(16, dtype=np.float32))[None, :], (128, 16)).copy()
        m = {"xin": xin, "posf": posf,
             "sa": np.ascontiguousarray(sa[:, c * NS:(c + 1) * NS].reshape(2, NS * 30, DA)),
             "sb": np.ascontiguousarray(sb[:, c * NS:(c + 1) * NS].reshape(2, NS * 2, DA)),
             "spl": np.ascontiguousarray(spl[:, c * NS:(c + 1) * NS].reshape(2, NS * 15, D))}
        m.update(wts)
        in_maps.append(m)
    return in_maps


def gather(results):
    y_prompt = np.zeros((4, 2048, D), np.float32)
    y_sample = np.zeros((128, 4, D), np.float32)
    nca_p = np.zeros((2, 4, 30, DA), np.float32)
    ncb_p = np.zeros((2, 4, 2, DA), np.float32)
    npl_p = np.zeros((2, 4, 15, D), np.float32)
    nca_s = np.zeros((2, 128, 30, DA), np.float32)
    ncb_s = np.zeros((2, 128, 2, DA), np.float32)
    npl_s = np.zeros((2, 128, 15, D), np.float32)
    for c in range(8):
        r = results[c]
        b, half = c // 2, c % 2
        yo = np.asarray(r["yout"])
        if half == 0:
            y_prompt[b, 0:PP - 16] = yo[16:PP]
        else:
            y_prompt[b, PP - 16:] = yo[HALO:PP]
            nca_p[:, b] = np.asarray(r["nap"])
            ncb_p[:, b] = np.asarray(r["nbp"])
            npl_p[:, b] = np.asarray(r["npp"])
        y_sample[c * NS:(c + 1) * NS] = yo[S0:SE].reshape(NS, LS, D)
        nca_s[:, c * NS:(c + 1) * NS] = np.asarray(r["nas"]).reshape(2, NS, 30, DA)
        ncb_s[:, c * NS:(c + 1) * NS] = np.asarray(r["nbs"]).reshape(2, NS, 2, DA)
        npl_s[:, c * NS:(c + 1) * NS] = np.asarray(r["nps"]).reshape(2, NS, 15, D)
    return (y_prompt, y_sample, nca_p, ncb_p, npl_p, nca_s, ncb_s, npl_s)


_NC_CACHE = {}


def kernel(**inputs):
    stage = int(os.environ.get("K_STAGE", "99"))
    if stage not in _NC_CACHE:
        _NC_CACHE[stage] = build_nc(stage)
    nc = _NC_CACHE[stage]
    in_maps = make_in_maps(inputs)
    res = run_bass_kernel_spmd(nc, in_maps, core_ids=list(range(8)))
    return gather(res.results)
```

```python
import os
import contextlib
import numpy as np
import concourse.bass as bass
import concourse.mybir as mybir
from concourse.bass_utils import run_bass_kernel_spmd

F32 = mybir.dt.float32
BF16 = mybir.dt.bfloat16
AF = mybir.ActivationFunctionType
ALU = mybir.AluOpType

D = 2048
KC = 16
T = 1152
TT = 384
NTT = 3
PP = 1077
HALO = 90
NS = 16
LS = 4
S0 = PP
SE = PP + NS * LS
DA = 1024
DFF = 8192
EPS = 1e-6
NW = 4
NLANE = 8
HAZ = int(os.environ.get("K_HAZ", "1024"))
SAME_SYNC_ENGS = set(os.environ.get("K_SAME_SYNC", "act,pool").split(","))


class Op:
    __slots__ = ("eng", "fn", "deps", "signal", "sigval", "lane", "laneval", "idx", "size", "cum")

    def __init__(self, eng, fn):
        self.eng = eng
        self.fn = fn
        self.deps = set()
        self.signal = False
        self.sigval = 0
        self.lane = None
        self.laneval = 0
        self.idx = 0
        self.size = 0
        self.cum = 0


class Prog:
    ENGS = ("pe", "act", "dve", "pool", "sp")

    def __init__(self):
        self.ops = []
        self.acc = {}
        self.lane_cnt = {}
        self.cum = {}

    def add(self, eng, fn, reads=(), writes=(), lane=None, size=0):
        op = Op(eng, fn)
        op.idx = len(self.ops)
        op.size = size
        c = self.cum.get(eng, 0)
        op.cum = c
        self.cum[eng] = c + size + 60
        deps = op.deps
        for (k, lo, hi) in reads:
            a = self.acc.get(k)
            if a is None:
                continue
            for (l2, h2, p) in a[0]:
                if l2 < hi and lo < h2:
                    deps.add(p)
        for (k, lo, hi) in writes:
            a = self.acc.get(k)
            if a is None:
                continue
            for (l2, h2, p) in a[0]:
                if l2 < hi and lo < h2:
                    deps.add(p)
            for (l2, h2, p) in a[1]:
                if l2 < hi and lo < h2:
                    deps.add(p)
        isdma = lane is not None
        for (k, lo, hi) in reads:
            a = self.acc.setdefault(k, [[], []])
            if not isdma:
                a[1] = [e for e in a[1] if not (e[2].lane is None and e[2].eng == eng and lo <= e[0] and e[1] <= hi)]
            a[1].append((lo, hi, op))
        for (k, lo, hi) in writes:
            a = self.acc.setdefault(k, [[], []])
            a[0] = [e for e in a[0] if not (lo <= e[0] and e[1] <= hi)]
            a[1] = [e for e in a[1] if not (lo <= e[0] and e[1] <= hi)]
            a[0].append((lo, hi, op))
        if isdma:
            n = self.lane_cnt.get(lane, 0) + 1
            self.lane_cnt[lane] = n
            op.lane = lane
            op.laneval = 16 * n
        self.ops.append(op)
        return op

    @staticmethod
    def need_same_sync(op, d):
        if op.eng == "pe":
            return False
        if op.lane is not None or op.eng in SAME_SYNC_ENGS:
            return True
        if op.eng == "dve":
            gap = d.size + (op.cum - d.cum - d.size - 60)
            return gap < HAZ
        return False

    def finalize(self):
        for op in self.ops:
            for d in op.deps:
                if d.lane is not None:
                    continue
                if d.eng == op.eng and not self.need_same_sync(op, d):
                    continue
                d.signal = True
        cnt = {e: 0 for e in self.ENGS}
        for op in self.ops:
            if op.signal:
                cnt[op.eng] += 1
                op.sigval = cnt[op.eng]

    def schedule(self):
        per = {e: [o for o in self.ops if o.eng == e] for e in self.ENGS}
        final_lanes = dict(self.lane_cnt)
        acts = {}
        for ename in self.ENGS:
            lst = []
            waited = {}

            def wait(sem, val):
                if waited.get(sem, 0) < val:
                    lst.append(("wait", sem, val))
                    waited[sem] = val

            for op in per[ename]:
                if op.lane is not None and op.laneval > 16:
                    wait("L" + op.lane, op.laneval - 16)
                for d in sorted(op.deps, key=lambda o: o.idx):
                    if d.lane is not None:
                        wait("L" + d.lane, d.laneval)
                    else:
                        if d.eng == ename and not self.need_same_sync(op, d):
                            continue
                        wait("E" + d.eng, d.sigval)
                if op.lane is not None:
                    lst.append(("op", op, "L" + op.lane, 16))
                elif op.signal:
                    lst.append(("op", op, "E" + ename, 1))
                else:
                    lst.append(("op", op, None, 0))
            if ename == "sp":
                for ln, n in final_lanes.items():
                    wait("L" + ln, 16 * n)
                for en in self.ENGS:
                    last = [o for o in per[en] if o.signal]
                    if last and en != ename:
                        wait("E" + en, last[-1].sigval)
            acts[ename] = lst
        return acts

    def simulate(self, acts):
        semv = {}
        pc = {e: 0 for e in self.ENGS}
        progress = True
        while progress:
            progress = False
            for e in self.ENGS:
                lst = acts[e]
                while pc[e] < len(lst):
                    a = lst[pc[e]]
                    if a[0] == "wait":
                        if semv.get(a[1], 0) < a[2]:
                            break
                    else:
                        if a[2] is not None:
                            semv[a[2]] = semv.get(a[2], 0) + a[3]
                    pc[e] += 1
                    progress = True
        stuck = {e: (pc[e], len(acts[e]), acts[e][pc[e]] if pc[e] < len(acts[e]) else None) for e in self.ENGS}
        ok = all(pc[e] == len(acts[e]) for e in self.ENGS)
        return ok, stuck, semv

    def emit(self, nc, sems, lane_sems):
        acts = self.schedule()
        ok, stuck, semv = self.simulate(acts)
        if not ok:
            raise RuntimeError(f"sync deadlock in schedule: {stuck}")

        def semof(name):
            return lane_sems[name[1:]] if name[0] == "L" else sems[name[1:]]

        def body(ename):
            def run(e):
                for a in acts[ename]:
                    if a[0] == "wait":
                        e.wait_ge(semof(a[1]), a[2])
                    else:
                        ins = a[1].fn(e)
                        if a[2] is not None:
                            ins.then_inc(semof(a[2]), a[3])
            return run

        with nc.Block() as block:
            block.tensor(body("pe"))
            block.scalar(body("act"))
            block.vector(body("dve"))
            block.gpsimd(body("pool"))
            block.sync(body("sp"))


def build_nc(stage=99):
    nc = bass.Bass("TRN2", target_bir_lowering=False)
    P = Prog()

    def din(name, shape):
        return nc.dram_tensor(name, list(shape), F32, kind="ExternalInput").ap()

    def dout(name, shape):
        return nc.dram_tensor(name, list(shape), F32, kind="ExternalOutput").ap()

    xin = din("xin", [T, D])
    posf = din("posf", [128, 16])
    sa = din("sa", [2, NS * 30, DA])
    sb = din("sb", [2, NS * 2, DA])
    spl = din("spl", [2, NS * 15, D])
    norm_mix = din("norm_mix", [4, D])
    norm_mlp = din("norm_mlp", [4, D])
    norm_final = din("norm_final", [D])
    big = stage > 0
    w_in = din("w_in_even", [2, D, 5120] if big else [1, 1])
    conv_a_w = din("conv_a_w", [2, 31, DA])
    conv_a_b = din("conv_a_b", [2, DA])
    ln_a_g = din("ln_a_g", [2, DA])
    ln_a_b = din("ln_a_b", [2, DA])
    conv_b_w = din("conv_b_w", [2, 3, DA])
    w_out = din("w_out_even", [2, D, D] if big else [1, 1])
    pool_w = din("pool_w", [2, 4, 512, 512] if big else [1, 1])
    pool_scale = din("pool_scale", [2, D])
    w_up = din("w_mlp_up", [4, D, DFF] if big else [1, 1])
    w_down = din("w_mlp_down", [4, DFF, D] if big else [1, 1])

    yout = dout("yout", [T, D])
    nap = dout("nap", [2, 30, DA])
    nbp = dout("nbp", [2, 2, DA])
    npp = dout("npp", [2, 15, D])
    nas = dout("nas", [2, NS * 30, DA])
    nbs = dout("nbs", [2, NS * 2, DA])
    nps = dout("nps", [2, NS * 15, D])

    es = contextlib.ExitStack()
    with es:
        def sb_t(name, n, dt=F32):
            return es.enter_context(nc.sbuf_tensor(name, [128, n], dt))

        X = sb_t("X", KC * T)
        H = sb_t("H", KC * T, BF16)
        R1 = sb_t("R1", KC * T, BF16)
        WS = [sb_t(f"W{i}", 16 * 128, BF16) for i in range(NW)]
        RSTD = sb_t("RSTD", T)
        R2 = sb_t("R2", T)
        S1 = sb_t("S1", T)
        S2 = sb_t("S2", T)
        TA = sb_t("TA", 1408)
        TB = sb_t("TB", 1664)
        TC = sb_t("TC", 1408)
        CONST = sb_t("CONST", 768)
        IDENT = sb_t("IDENT", 128)
        ONES = sb_t("ONES", 128)
        FIX = sb_t("FIX", 64)
        POSF = sb_t("POSF", 16)
        DG = [sb_t(f"DG{i}", 128) for i in range(3)]
        STG = [sb_t(f"STG{i}", 512) for i in range(2)]
        GS = [sb_t(f"GS{i}", 64) for i in range(2)]
        STO = [sb_t(f"STO{i}", 256) for i in range(2)]
        PS = [es.enter_context(nc.psum_tensor(f"ps{i}", [128, 512], F32)) for i in range(8)]
        sems = {e: es.enter_context(nc.semaphore(f"sem_{e}")) for e in Prog.ENGS}
        lane_sems = {}
        for i in range(NW):
            lane_sems[f"w{i}"] = es.enter_context(nc.semaphore(f"lw{i}"))
        for i in range(NLANE):
            lane_sems[f"s{i}"] = es.enter_context(nc.semaphore(f"ls{i}"))

        R1f = R1[:, :].bitcast(F32)

        sp_lane = [0]

        def sp_dma(out, in_, reads=(), writes=()):
            ln = f"s{sp_lane[0] % NLANE}"
            sp_lane[0] += 1
            return P.add("sp", lambda e, o=out, i=in_: e.dma_start(out=o, in_=i), reads, writes, lane=ln)

        def Xc(c, lo=0, hi=T):
            return X[:, c * T + lo:c * T + hi]

        def Hc(c, lo=0, hi=T):
            return H[:, c * T + lo:c * T + hi]

        def kX(c, lo=0, hi=T):
            return ("X", c * T + lo, c * T + hi)

        def kH(c, lo=0, hi=T):
            return ("H", c * T + lo, c * T + hi)

        def kR1b(lo, hi):
            return ("R1", lo, hi)

        def fsz(ap):
            n = 1
            for d_ in ap.shape[1:]:
                n *= d_
            return n

        def act_op(out, in_, func, reads, writes, scale=None, bias=None):
            kw = {}
            if scale is not None:
                kw["scale"] = scale
            if bias is not None:
                kw["bias"] = bias
            return P.add("act", lambda e: e.activation(out=out, in_=in_, func=func, **kw), reads, writes, size=fsz(out))

        def tt_op(eng, out, in0, in1, op, reads, writes):
            return P.add(eng, lambda e: e.tensor_tensor(out=out, in0=in0, in1=in1, op=op), reads, writes, size=fsz(out))

        def ts_op(eng, out, in0, s1, s2, op0, op1, reads, writes):
            if s2 is None:
                s2, op1 = 0.0, ALU.add
            return P.add(eng, lambda e: e.tensor_scalar(out=out, in0=in0, scalar1=s1, scalar2=s2, op0=op0, op1=op1), reads, writes, size=fsz(out))

        def stt_op(eng, out, in0, scalar, in1, op0, op1, reads, writes):
            return P.add(eng, lambda e: e.scalar_tensor_tensor(out=out, in0=in0, scalar=scalar, in1=in1, op0=op0, op1=op1), reads, writes, size=fsz(out))

        def copy_op(eng, out, in_, reads, writes):
            if eng == "act":
                return P.add("act", lambda e: e.copy(out=out, in_=in_), reads, writes, size=fsz(out))
            return P.add(eng, lambda e: e.tensor_copy(out=out, in_=in_), reads, writes, size=fsz(out))

        def memset_op(eng, ap, val, writes):
            return P.add(eng, lambda e: e.memset(ap, val), (), writes)

        def mm_op(out, lhsT, rhs, start, stop, reads, writes):
            return P.add("pe", lambda e: e.matmul(out, lhsT, rhs, start=start, stop=stop), reads, writes)

        def tr_op(out, in_, reads, writes, n):
            return P.add("pe", lambda e: e.transpose(out, in_, IDENT[0:n, 0:n] if False else IDENT[:, :]), reads, writes)

        job_ctr = [0]

        def next_slot():
            s = job_ctr[0] % 2
            job_ctr[0] += 1
            return s

        def ps_tile(slot, tt):
            return PS[slot * 3 + tt][:, 0:TT]

        def kps(slot, tt):
            return (f"PS{slot * 3 + tt}", 0, TT)

        misc_ctr = [0]

        def next_misc():
            b = 6 + (misc_ctr[0] % 2)
            misc_ctr[0] += 1
            return b

        units = []

        def plan_units():
            for l in range(4):
                if l % 2 == 0:
                    e = l // 2
                    wv = w_in[e].rearrange("(k p) n -> p k n", p=128)
                    order = []
                    for c in range(8):
                        order += [8 + c, c]
                    for c in range(8):
                        order += [24 + c, 32 + c, 16 + c]
                    for blk in order:
                        units.append((wv[:, :, blk * 128:(blk + 1) * 128], 16))
                    wo = w_out[e].rearrange("(k p) n -> p k n", p=128)
                    for m in range(16):
                        units.append((wo[:, :, m * 128:(m + 1) * 128], 16))
                else:
                    o = l // 2
                    for g in range(4):
                        pw = pool_w[o, g].rearrange("(k p) n -> p k n", p=128)
                        for m in range(4):
                            units.append((pw[:, :, m * 128:(m + 1) * 128], 4))
                wu = w_up[l].rearrange("(k p) n -> p k n", p=128)
                for gi in range(4):
                    for fj in range(16):
                        f = gi * 16 + fj
                        units.append((wu[:, :, f * 128:(f + 1) * 128], 16))
                    wd = w_down[l][gi * 2048:(gi + 1) * 2048, :].rearrange("(k p) n -> p k n", p=128)
                    for m in range(16):
                        units.append((wd[:, :, m * 128:(m + 1) * 128], 16))

        if big:
            plan_units()
        unit_issued = [0]
        unit_next = [0]

        def issue_unit(u):
            if u >= len(units):
                return
            src, nk = units[u]
            slot = u % NW
            dst = WS[slot][:, 0:nk * 128].rearrange("p (k n) -> p k n", n=128)
            P.add("pool", lambda e, o=dst, i=src: e.dma_start(out=o, in_=i), (), [(f"W{slot}", 0, nk * 128)], lane=f"w{slot}")

        def take_unit():
            u = unit_next[0]
            unit_next[0] += 1
            while unit_issued[0] < min(u + NW, len(units)):
                issue_unit(unit_issued[0])
                unit_issued[0] += 1
            slot = u % NW
            nk = units[u][1]
            return slot, nk

        def mm_job(rhs_fn, rhs_key_fn):
            wslot, nk = take_unit()
            slot = next_slot()
            W3 = WS[wslot][:, 0:nk * 128].rearrange("p (k n) -> p k n", n=128)
            for k in range(nk):
                for tt in range(NTT):
                    mm_op(ps_tile(slot, tt), W3[:, k, :], rhs_fn(k, tt), k == 0, k == nk - 1,
                          [(f"W{wslot}", k * 128, (k + 1) * 128), rhs_key_fn(k, tt)], [kps(slot, tt)])
            return slot

        cst_src = [
            ("gmix", norm_mix.rearrange("l (c p) -> (l c) p", p=128), 64),
            ("gmlp", norm_mlp.rearrange("l (c p) -> (l c) p", p=128), 64),
            ("gfin", norm_final.rearrange("(c p) -> c p", p=128), 16),
            ("caw", conv_a_w.rearrange("e k (c p) -> (e k c) p", p=128), 496),
            ("cab", conv_a_b.rearrange("e (c p) -> (e c) p", p=128), 16),
            ("lng", ln_a_g.rearrange("e (c p) -> (e c) p", p=128), 16),
            ("lnb", ln_a_b.rearrange("e (c p) -> (e c) p", p=128), 16),
            ("cbw", conv_b_w.rearrange("e k (c p) -> (e k c) p", p=128), 48),
            ("psc", pool_scale.rearrange("o (c p) -> (o c) p", p=128), 32),
        ]
        coff = {}
        row = 0
        for name, ap, n in cst_src:
            coff[name] = row
            r = 0
            while r < n:
                blk = (row + r) // 128
                p0 = (row + r) % 128
                cnt = min(n - r, 128 - p0)
                sp_dma(R1f[p0:p0 + cnt, blk * 128:(blk + 1) * 128], ap[r:r + cnt, :], (),
                       [("R1", blk * 512, (blk + 1) * 512)])
                r += cnt
            row += n
        assert row == 768
        memset_op("pool", ONES[:, :], 1.0, [("ONES", 0, 128)])
        P.add("pool", lambda e: e.affine_select(out=IDENT[:, :], in_=ONES[:, :], pattern=[[1, 128]],
                                                compare_op=ALU.is_equal, fill=0.0, base=0, channel_multiplier=-1),
              [("ONES", 0, 128)], [("IDENT", 0, 128)])
        for blk in range(6):
            b = next_misc()
            P.add("pe", lambda e, b=b, blk=blk: e.transpose(PS[b][:, 0:128], R1f[:, blk * 128:(blk + 1) * 128], IDENT[:, :]),
                  [("R1", blk * 512, (blk + 1) * 512), ("IDENT", 0, 128)], [(f"PS{b}", 0, 128)])
            copy_op("dve", CONST[:, blk * 128:(blk + 1) * 128], PS[b][:, 0:128], [(f"PS{b}", 0, 128)],
                    [("CONST", blk * 128, (blk + 1) * 128)])

        def ccol(name, idx):
            o = coff[name] + idx
            return CONST[:, o:o + 1]

        def kc():
            return ("CONST", 0, 768)

        sp_dma(POSF[:, :], posf[:, :], (), [("POSF", 0, 16)])
        for g in range(4):
            w = float(2 ** (g + 1))
            fx = FIX[:, g * 16:(g + 1) * 16]
            ts_op("dve", fx, POSF[:, :], 1.0, w, ALU.add, ALU.min, [("POSF", 0, 16)], [("FIX", g * 16, g * 16 + 16)])
            P.add("dve", lambda e, fx=fx: e.reciprocal(out=fx, in_=fx), [("FIX", g * 16, g * 16 + 16)], [("FIX", g * 16, g * 16 + 16)])
            ts_op("dve", fx, fx, w, None, ALU.mult, None, [("FIX", g * 16, g * 16 + 16)], [("FIX", g * 16, g * 16 + 16)])

        def stage4(i):
            return R1f[:, i * 2048:(i + 1) * 2048]

        def kstage4(i, lo=0, hi=2048):
            return ("R1", i * 8192 + lo * 4, i * 8192 + hi * 4)

        cp_alt = [0]

        def alt_eng():
            cp_alt[0] += 1
            return "act" if cp_alt[0] % 2 else "dve"

        for blk in range(T // 128):
            si = blk % 4
            sp_dma(stage4(si), xin[blk * 128:(blk + 1) * 128, :], (), [kstage4(si)])
            for cg in range(4):
                b = (blk * 4 + cg) % 8
                for j in range(4):
                    c = cg * 4 + j
                    P.add("pe", lambda e, b=b, j=j, c=c, si=si: e.transpose(PS[b][:, j * 128:(j + 1) * 128], stage4(si)[:, c * 128:(c + 1) * 128], IDENT[:, :]),
                          [kstage4(si, c * 128, (c + 1) * 128), ("IDENT", 0, 128)], [(f"PS{b}", j * 128, (j + 1) * 128)])
                outv = X[:, :].rearrange("p (c t) -> p c t", t=T)[:, cg * 4:(cg + 1) * 4, blk * 128:(blk + 1) * 128]
                inv = PS[b][:, :].rearrange("p (c t) -> p c t", t=128)
                copy_op(alt_eng(), outv, inv, [(f"PS{b}", 0, 512)], [kX(c, blk * 128, (blk + 1) * 128) for c in range(cg * 4, cg * 4 + 4)])

        DBG = os.environ.get("K_DBG", "")
        for e_ in range(0 if "nod2d" in DBG else 2):
            sp_dma(nas[e_].rearrange("(s r) c -> s r c", r=30)[:, 0:26, :],
                   sa[e_].rearrange("(s r) c -> s r c", r=30)[:, 4:30, :])
            sp_dma(nps[e_].rearrange("(s r) c -> s r c", r=15)[:, 0:11, :],
                   spl[e_].rearrange("(s r) c -> s r c", r=15)[:, 4:15, :])

        def nchunk(spec, c, add_eng="dve"):
            gname, gidx, make_h = spec
            if c == 0:
                act_op(S2[:, :], Xc(c), AF.Square, [kX(c)], [("S2", 0, T)])
            else:
                tmp, tk = (S1, "S1") if c % 2 else (R2, "R2")
                act_op(tmp[:, 0:T], Xc(c), AF.Square, [kX(c)], [(tk, 0, T)])
                tt_op(add_eng, S2[:, :], S2[:, :], tmp[:, 0:T], ALU.add, [("S2", 0, T), (tk, 0, T)], [("S2", 0, T)])
            if make_h:
                act_op(Hc(c), Xc(c), AF.Copy, [kX(c), kc()], [kH(c)], scale=ccol(gname, gidx * 16 + c))

        def nfinish():
            slot = next_slot()
            for tt in range(NTT):
                mm_op(ps_tile(slot, tt), ONES[:, :], S2[:, tt * TT:(tt + 1) * TT], True, True,
                      [("ONES", 0, 128), ("S2", tt * TT, (tt + 1) * TT)], [kps(slot, tt)])
                rs = RSTD[:, tt * TT:(tt + 1) * TT]
                ts_op("dve", rs, ps_tile(slot, tt), 1.0 / D, EPS, ALU.mult, ALU.add, [kps(slot, tt)], [("RSTD", tt * TT, (tt + 1) * TT)])
                act_op(rs, rs, AF.Sqrt, [("RSTD", tt * TT, (tt + 1) * TT)], [("RSTD", tt * TT, (tt + 1) * TT)])
                P.add("dve", lambda e, rs=rs: e.reciprocal(out=rs, in_=rs), [("RSTD", tt * TT, (tt + 1) * TT)], [("RSTD", tt * TT, (tt + 1) * TT)])

        def rms_prep(spec):
            for c in range(KC):
                nchunk(spec, c)
            nfinish()

        def mix_spec(l):
            if l >= nlayers:
                return ("gfin", 0, False)
            return ("gmix", l, l % 2 == 0)

        def mlp(l):
            nxt = mix_spec(l + 1)
            for gi in range(4):
                for fj in range(16):
                    slot = mm_job(lambda k, tt: Hc(k, tt * TT, (tt + 1) * TT), lambda k, tt: kH(k, tt * TT, (tt + 1) * TT))
                    tmp, tk = (TA, "TA") if fj % 2 else (TC, "TC")
                    for tt in range(NTT):
                        tv = tmp[:, tt * TT:(tt + 1) * TT]
                        stt_op("dve", tv, ps_tile(slot, tt), 0.0, RSTD[:, tt * TT:(tt + 1) * TT], ALU.max, ALU.mult,
                               [kps(slot, tt), ("RSTD", tt * TT, (tt + 1) * TT)], [(tk, tt * TT, (tt + 1) * TT)])
                        lo = fj * T + tt * TT
                        act_op(R1[:, lo:lo + TT], tv, AF.Square, [(tk, tt * TT, (tt + 1) * TT)], [kR1b(lo * 2, (lo + TT) * 2)])
                for m in range(16):
                    slot = mm_job(lambda k, tt: R1[:, k * T + tt * TT:k * T + (tt + 1) * TT],
                                  lambda k, tt: kR1b((k * T + tt * TT) * 2, (k * T + (tt + 1) * TT) * 2))
                    for tt in range(NTT):
                        xv = Xc(m, tt * TT, (tt + 1) * TT)
                        tt_op("dve", xv, ps_tile(slot, tt), xv, ALU.add, [kps(slot, tt), kX(m, tt * TT, (tt + 1) * TT)],
                              [kX(m, tt * TT, (tt + 1) * TT)])
                    if gi == 3:
                        nchunk(nxt, m)
            nfinish()

        sto_ctr = [0]

        def state_out_prep(in_s, nt, key_in):
            i = sto_ctr[0] % 2
            sto_ctr[0] += 1
            gs = GS[i]
            copy_op("dve", gs[:, 0:nt * 16].rearrange("p (t s) -> p t s", s=16), in_s, [key_in], [(f"GS{i}", 0, 64)])
            return i

        def state_out_fin(i, in_p, np_, nt, key_in, dst_p, dst_s_fn):
            b = next_misc()
            sto = STO[i]
            gs = GS[i]
            P.add("pe", lambda e: e.transpose(PS[b][0:np_, 0:128], in_p, IDENT[:, :]), [key_in, ("IDENT", 0, 128)], [(f"PS{b}", 0, 256)])
            P.add("pe", lambda e: e.transpose(PS[b][0:nt * 16, 128:256], gs[:, 0:nt * 16], IDENT[:, :]), [(f"GS{i}", 0, 64), ("IDENT", 0, 128)], [(f"PS{b}", 0, 256)])
            copy_op("act", sto[0:np_, 0:128], PS[b][0:np_, 0:128], [(f"PS{b}", 0, 128)], [(f"STO{i}", 0, 128)])
            copy_op("act", sto[0:nt * 16, 128:256], PS[b][0:nt * 16, 128:256], [(f"PS{b}", 128, 256)], [(f"STO{i}", 128, 256)])
            sp_dma(dst_p, sto[0:np_, 0:128], [(f"STO{i}", 0, 128)], ())
            for t in range(nt):
                sp_dma(dst_s_fn(t), sto[t * 16:(t + 1) * 16, 128:256], [(f"STO{i}", 128, 256)], ())

        def state_out(in_p, np_, in_s, nt, key_in, dst_p, dst_s_fn):
            i = state_out_prep(in_s, nt, key_in)
            state_out_fin(i, in_p, np_, nt, key_in, dst_p, dst_s_fn)

        stg_ctr = [0]

        def state_load(src_rows, nblk, rows_per_blk):
            i = stg_ctr[0] % 2
            stg_ctr[0] += 1
            stg = STG[i]
            for q in range(nblk):
                sp_dma(stg[0:rows_per_blk, q * 128:(q + 1) * 128], src_rows[q * rows_per_blk:(q + 1) * rows_per_blk, :], (),
                       [(f"STG{i}", q * 128, (q + 1) * 128)])
            return i

        def state_xpose(i, nblk, rows_per_blk, dst3, dkeys, hist):
            stg = STG[i]
            b = next_misc()
            for q in range(nblk):
                P.add("pe", lambda e, q=q: e.transpose(PS[b][:, q * rows_per_blk:(q + 1) * rows_per_blk], stg[0:rows_per_blk, q * 128:(q + 1) * 128], IDENT[0:rows_per_blk, 0:rows_per_blk]),
                      [(f"STG{i}", q * 128, (q + 1) * 128), ("IDENT", 0, 128)], [(f"PS{b}", q * rows_per_blk, (q + 1) * rows_per_blk)])
            n = nblk * rows_per_blk
            copy_op("act", dst3, PS[b][:, 0:n].rearrange("p (s r) -> p s r", r=hist), [(f"PS{b}", 0, n)], dkeys)

        def state_in(src_rows, nblk, rows_per_blk, dst3, dkeys, hist):
            i = state_load(src_rows, nblk, rows_per_blk)
            state_xpose(i, nblk, rows_per_blk, dst3, dkeys, hist)

        def even_mixer(l):
            e_ = l // 2
            AOFF = 30
            ASO = 30 + PP
            TBs = TB[:, ASO:ASO + NS * 34].rearrange("p (s j) -> p s j", j=34)
            kTB = ("TB", 0, 1664)
            memset_op("pool", TB[:, 0:30], 0.0, [("TB", 0, 30)])
            sav = sa[e_]
            Hrhs = (lambda k, tt: Hc(k, tt * TT, (tt + 1) * TT), lambda k, tt: kH(k, tt * TT, (tt + 1) * TT))

            def a_jobs_gate():
                s_gate = mm_job(*Hrhs)
                for tt in range(NTT):
                    copy_op("act", TA[:, tt * TT:(tt + 1) * TT], ps_tile(s_gate, tt), [kps(s_gate, tt)], [("TA", tt * TT, (tt + 1) * TT)])

            def a_gate_sig():
                tt_op("dve", TA[:, 0:T], TA[:, 0:T], RSTD[:, :], ALU.mult, [("TA", 0, T), ("RSTD", 0, T)], [("TA", 0, T)])
                act_op(TA[:, 0:T], TA[:, 0:T], AF.Sigmoid, [("TA", 0, T)], [("TA", 0, T)])

            NPE = int(os.environ.get("K_NPE", "10"))
            dg_ctr = [0]

            def conv_pe(c):
                slot = next_slot()
                taps = list(range(31 - NPE, 31))
                for j, k in enumerate(taps):
                    di = dg_ctr[0] % 3
                    dg_ctr[0] += 1
                    wk = ccol("caw", (e_ * 31 + k) * 8 + c)
                    act_op(DG[di][:, :], IDENT[:, :], AF.Copy, [("IDENT", 0, 128), kc()], [(f"DG{di}", 0, 128)], scale=wk)
                    for tt in range(NTT):
                        ncol = TT if tt < 2 else PP - 2 * TT
                        mm_op(PS[slot * 3 + tt][:, 0:ncol], DG[di][:, :], TB[:, k + tt * TT:k + tt * TT + ncol], j == 0, j == len(taps) - 1,
                              [(f"DG{di}", 0, 128), kTB], [kps(slot, tt)])
                return slot

            ld = state_load(sav[:, 0:128], 4, 120)
            a_jobs_gate()
            a_gate_sig()
            s_val = mm_job(*Hrhs)
            for c in range(8):
                ld_next = state_load(sav[:, (c + 1) * 128:(c + 2) * 128], 4, 120) if c + 1 < 8 else None
                state_xpose(ld, 4, 120, TBs[:, :, 0:30], [("TB", ASO, ASO + NS * 34)], 30)
                ld = ld_next
                tt_op("dve", TA[:, 0:T], TA[:, 0:T], RSTD[:, :], ALU.mult, [("TA", 0, T), ("RSTD", 0, T)], [("TA", 0, T)])
                for tt in range(2):
                    tt_op("dve", TB[:, AOFF + tt * TT:AOFF + (tt + 1) * TT], ps_tile(s_val, tt), TA[:, tt * TT:(tt + 1) * TT], ALU.mult,
                          [kps(s_val, tt), ("TA", tt * TT, (tt + 1) * TT)], [("TB", AOFF + tt * TT, AOFF + (tt + 1) * TT)])
                np2 = PP - 2 * TT
                tt_op("dve", TB[:, AOFF + 2 * TT:AOFF + PP], PS[s_val * 3 + 2][:, 0:np2], TA[:, 2 * TT:PP], ALU.mult,
                      [kps(s_val, 2), ("TA", 2 * TT, PP)], [("TB", AOFF + 2 * TT, AOFF + PP)])
                tt_op("dve", TBs[:, :, 30:34], PS[s_val * 3 + 2][:, np2:np2 + 64].rearrange("p (s j) -> p s j", j=4),
                      TA[:, S0:SE].rearrange("p (s j) -> p s j", j=4), ALU.mult,
                      [kps(s_val, 2), ("TA", S0, SE)], [("TB", ASO, ASO + NS * 34)])
                state_out(TB[:, AOFF + PP - 30:AOFF + PP], 30, TBs[:, :, 30:34].rearrange("p s t -> p t s"), 4, kTB,
                          nap[e_, :, c * 128:(c + 1) * 128],
                          lambda t, c=c: nas[e_].rearrange("(s r) n -> s r n", r=30)[:, 26 + t, c * 128:(c + 1) * 128])
                if c + 1 < 8:
                    a_jobs_gate()
                    s_val_next = mm_job(*Hrhs)
                s_cv = conv_pe(c) if NPE > 0 else None
                acc = R1f[:, c * T:(c + 1) * T]
                kacc = kR1b(c * T * 4, (c + 1) * T * 4)
                accs = acc[:, S0:SE].rearrange("p (s j) -> p s j", j=4)
                memset_op("pool", acc[:, SE:T], 0.0, [kR1b((c * T + SE) * 4, (c + 1) * T * 4)])
                for k in range(31):
                    wk = ccol("caw", (e_ * 31 + k) * 8 + c)
                    if k == 0:
                        ts_op("dve", acc[:, 0:PP], TB[:, 0:PP], wk, ccol("cab", e_ * 8 + c), ALU.mult, ALU.add,
                              [kTB, kc()], [kR1b(c * T * 4, (c * T + PP) * 4)])
                        ts_op("dve", accs, TBs[:, :, 0:4], wk, ccol("cab", e_ * 8 + c), ALU.mult, ALU.add,
                              [kTB, kc()], [kR1b((c * T + S0) * 4, (c * T + SE) * 4)])
                    else:
                        if k < 31 - NPE:
                            stt_op("dve", acc[:, 0:PP], TB[:, k:k + PP], wk, acc[:, 0:PP], ALU.mult, ALU.add,
                                   [kTB, kc(), kR1b(c * T * 4, (c * T + PP) * 4)], [kR1b(c * T * 4, (c * T + PP) * 4)])
                        stt_op("dve", accs, TBs[:, :, k:k + 4], wk, accs, ALU.mult, ALU.add,
                               [kTB, kc(), kR1b((c * T + S0) * 4, (c * T + SE) * 4)], [kR1b((c * T + S0) * 4, (c * T + SE) * 4)])
                if c + 1 < 8:
                    a_gate_sig()
                if s_cv is not None:
                    for tt in range(NTT):
                        ncol = TT if tt < 2 else PP - 2 * TT
                        av = acc[:, tt * TT:tt * TT + ncol]
                        ka = kR1b((c * T + tt * TT) * 4, (c * T + tt * TT + ncol) * 4)
                        tt_op("dve", av, PS[s_cv * 3 + tt][:, 0:ncol], av, ALU.add, [kps(s_cv, tt), ka], [ka])
                if c == 0:
                    act_op(S2[:, :], acc, AF.Square, [kacc], [("S2", 0, T)])
                    copy_op("dve", S1[:, :], acc, [kacc], [("S1", 0, T)])
                else:
                    act_op(TC[:, 0:T], acc, AF.Square, [kacc], [("TC", 0, T)])
                    tt_op("dve", S1[:, :], S1[:, :], acc, ALU.add, [("S1", 0, T), kacc], [("S1", 0, T)])
                    tt_op("dve", S2[:, :], S2[:, :], TC[:, 0:T], ALU.add, [("S2", 0, T), ("TC", 0, T)], [("S2", 0, T)])
                if c + 1 < 8:
                    s_val = s_val_next
            s_1 = next_slot()
            for tt in range(NTT):
                mm_op(ps_tile(s_1, tt), ONES[:, :], S1[:, tt * TT:(tt + 1) * TT], True, True,
                      [("ONES", 0, 128), ("S1", tt * TT, (tt + 1) * TT)], [kps(s_1, tt)])
            s_2 = next_slot()
            for tt in range(NTT):
                mm_op(ps_tile(s_2, tt), ONES[:, :], S2[:, tt * TT:(tt + 1) * TT], True, True,
                      [("ONES", 0, 128), ("S2", tt * TT, (tt + 1) * TT)], [kps(s_2, tt)])
            for tt in range(NTT):
                sl = slice(tt * TT, (tt + 1) * TT)
                kM = ("R2", tt * TT, (tt + 1) * TT)
                kV = ("S2", tt * TT, (tt + 1) * TT)
                ts_op("dve", R2[:, sl], ps_tile(s_1, tt), 1.0 / DA, None, ALU.mult, None, [kps(s_1, tt)], [kM])
                tt_op("dve", S2[:, sl], R2[:, sl], R2[:, sl], ALU.mult, [kM], [kV])
                stt_op("dve", S2[:, sl], ps_tile(s_2, tt), 1.0 / DA, S2[:, sl], ALU.mult, ALU.subtract, [kps(s_2, tt), kV], [kV])
                ts_op("dve", S2[:, sl], S2[:, sl], 0.0, EPS, ALU.max, ALU.add, [kV], [kV])
                act_op(S2[:, sl], S2[:, sl], AF.Sqrt, [kV], [kV])
                P.add("dve", lambda e, v=S2[:, sl]: e.reciprocal(out=v, in_=v), [kV], [kV])
                stt_op("dve", R2[:, sl], R2[:, sl], -1.0, S2[:, sl], ALU.mult, ALU.mult, [kM, kV], [kM])

            def ln_apply(c):
                acc = R1f[:, c * T:(c + 1) * T]
                kacc = kR1b(c * T * 4, (c + 1) * T * 4)
                tt_op("dve", acc, acc, S2[:, 0:T], ALU.mult, [kacc, ("S2", 0, T)], [kacc])
                tt_op("dve", acc, acc, R2[:, 0:T], ALU.add, [kacc, ("R2", 0, T)], [kacc])
                act_op(R1[:, c * 2 * T:c * 2 * T + T], acc, AF.Silu, [kacc, kc()], [kacc],
                       scale=ccol("lng", e_ * 8 + c), bias=ccol("lnb", e_ * 8 + c))
            VOFF = 2
            VSO = 2 + PP
            TBv = TB[:, VSO:VSO + NS * 6].rearrange("p (s j) -> p s j", j=6)
            memset_op("pool", TB[:, 0:2], 0.0, [("TB", 0, 2)])
            memset_op("pool", S1[:, SE:T], 0.0, [("S1", SE, T)])
            sbv = sb[e_]
            ldb = state_load(sbv[:, 0:128], 1, 32)
            for c in range(8):
                ldb_next = state_load(sbv[:, (c + 1) * 128:(c + 2) * 128], 1, 32) if c + 1 < 8 else None
                state_xpose(ldb, 1, 32, TBv[:, :, 0:2], [("TB", VSO, VSO + NS * 6)], 2)
                ldb = ldb_next
                s_cg = mm_job(lambda k, tt: Hc(k, tt * TT, (tt + 1) * TT), lambda k, tt: kH(k, tt * TT, (tt + 1) * TT))
                ln_apply(c)
                for tt in range(NTT):
                    tt_op("dve", TA[:, tt * TT:(tt + 1) * TT], ps_tile(s_cg, tt), RSTD[:, tt * TT:(tt + 1) * TT], ALU.mult,
                          [kps(s_cg, tt), ("RSTD", tt * TT, (tt + 1) * TT)], [("TA", tt * TT, (tt + 1) * TT)])
                tt_op("dve", TA[:, 0:T], TA[:, 0:T], RSTD[:, :], ALU.mult, [("TA", 0, T), ("RSTD", 0, T)], [("TA", 0, T)])
                s_bx = mm_job(lambda k, tt: Hc(k, tt * TT, (tt + 1) * TT), lambda k, tt: kH(k, tt * TT, (tt + 1) * TT))
                for tt in range(2):
                    tt_op("dve", TB[:, VOFF + tt * TT:VOFF + (tt + 1) * TT], ps_tile(s_bx, tt), TA[:, tt * TT:(tt + 1) * TT], ALU.mult,
                          [kps(s_bx, tt), ("TA", tt * TT, (tt + 1) * TT)], [("TB", VOFF + tt * TT, VOFF + (tt + 1) * TT)])
                np2 = PP - 2 * TT
                tt_op("dve", TB[:, VOFF + 2 * TT:VOFF + PP], PS[s_bx * 3 + 2][:, 0:np2], TA[:, 2 * TT:PP], ALU.mult,
                      [kps(s_bx, 2), ("TA", 2 * TT, PP)], [("TB", VOFF + 2 * TT, VOFF + PP)])
                tt_op("dve", TBv[:, :, 2:6], PS[s_bx * 3 + 2][:, np2:np2 + 64].rearrange("p (s j) -> p s j", j=4),
                      TA[:, S0:SE].rearrange("p (s j) -> p s j", j=4), ALU.mult,
                      [kps(s_bx, 2), ("TA", S0, SE)], [("TB", VSO, VSO + NS * 6)])
                so_i = state_out_prep(TBv[:, :, 4:6].rearrange("p s t -> p t s"), 2, kTB)
                S1s = S1[:, S0:SE].rearrange("p (s j) -> p s j", j=4)
                for k in range(3):
                    wk = ccol("cbw", (e_ * 3 + k) * 8 + c)
                    if k == 0:
                        ts_op("dve", S1[:, 0:PP], TB[:, 0:PP], wk, None, ALU.mult, None, [kTB, kc()], [("S1", 0, PP)])
                        ts_op("dve", S1s, TBv[:, :, 0:4], wk, None, ALU.mult, None, [kTB, kc()], [("S1", S0, SE)])
                    else:
                        stt_op("dve", S1[:, 0:PP], TB[:, k:k + PP], wk, S1[:, 0:PP], ALU.mult, ALU.add, [kTB, kc(), ("S1", 0, PP)], [("S1", 0, PP)])
                        stt_op("dve", S1s, TBv[:, :, k:k + 4], wk, S1s, ALU.mult, ALU.add, [kTB, kc(), ("S1", S0, SE)], [("S1", S0, SE)])
                s_bg = mm_job(lambda k, tt: Hc(k, tt * TT, (tt + 1) * TT), lambda k, tt: kH(k, tt * TT, (tt + 1) * TT))
                for tt in range(NTT):
                    tt_op("dve", TC[:, tt * TT:(tt + 1) * TT], ps_tile(s_bg, tt), RSTD[:, tt * TT:(tt + 1) * TT], ALU.mult,
                          [kps(s_bg, tt), ("RSTD", tt * TT, (tt + 1) * TT)], [("TC", tt * TT, (tt + 1) * TT)])
                bo = c * 2 * T + T
                tt_op("dve", R1[:, bo:bo + T], TC[:, 0:T], S1[:, :], ALU.mult, [("TC", 0, T), ("S1", 0, T)], [kR1b(bo * 2, (bo + T) * 2)])
                state_out_fin(so_i, TB[:, VOFF + PP - 2:VOFF + PP], 2, 2, kTB,
                          nbp[e_, :, c * 128:(c + 1) * 128],
                          lambda t, c=c: nbs[e_].rearrange("(s r) n -> s r n", r=2)[:, t, c * 128:(c + 1) * 128])
            for m in range(16):
                def rhs(k, tt):
                    base = (k % 8) * 2 * T + (T if k >= 8 else 0)
                    return R1[:, base + tt * TT:base + (tt + 1) * TT]

                def rkey(k, tt):
                    base = (k % 8) * 2 * T + (T if k >= 8 else 0)
                    return kR1b((base + tt * TT) * 2, (base + (tt + 1) * TT) * 2)
                slot = mm_job(rhs, rkey)
                for tt in range(NTT):
                    xv = Xc(m, tt * TT, (tt + 1) * TT)
                    tt_op("dve", xv, ps_tile(slot, tt), xv, ALU.add, [kps(slot, tt), kX(m, tt * TT, (tt + 1) * TT)], [kX(m, tt * TT, (tt + 1) * TT)])
                nchunk(("gmlp", l, True), m)
            nfinish()

        def odd_mixer(l):
            o_ = l // 2
            EO = 15
            ESO = 15 + PP
            NE = ESO
            bufs = {"TA": TA, "TB": TB, "TC": TC}

            def sview(buf):
                return buf[:, ESO:ESO + NS * 19].rearrange("p (s j) -> p s j", j=19)
            for nm in ("TA", "TB", "TC"):
                memset_op("pool", bufs[nm][:, 0:15], 0.0, [(nm, 0, 15)])
            spv = spl[o_]
            ld = state_load(spv[:, 0:128], 2, 120)
            for c in range(KC):
                g = c // 4
                w = 2 ** (g + 1)
                E, Es = TA, sview(TA)
                ld_next = state_load(spv[:, (c + 1) * 128:(c + 2) * 128], 2, 120) if c + 1 < KC else None
                state_xpose(ld, 2, 120, Es[:, :, 0:15], [("TA", ESO, ESO + NS * 19)], 15)
                ld = ld_next
                gcol = ccol("gmix", l * 16 + c)
                stt_op("dve", E[:, EO:EO + PP], Xc(c, 0, PP), gcol, RSTD[:, 0:PP], ALU.mult, ALU.mult,
                       [kX(c, 0, PP), kc(), ("RSTD", 0, PP)], [("TA", EO, EO + PP)])
                stt_op("dve", Es[:, :, 15:19], Xc(c, S0, SE).rearrange("p (s j) -> p s j", j=4), gcol,
                       RSTD[:, S0:SE].rearrange("p (s j) -> p s j", j=4), ALU.mult, ALU.mult,
                       [kX(c, S0, SE), kc(), ("RSTD", S0, SE)], [("TA", ESO, ESO + NS * 19)])
                src, srcn = E, "TA"
                pp = [("TB", TB), ("TC", TC)]
                sh = 1
                step = 0
                while sh < w:
                    dn, dst = pp[step % 2]
                    lo = 2 * sh - 1
                    tt_op("dve", dst[:, lo:NE], src[:, lo:NE], src[:, lo - sh:NE - sh], ALU.add,
                          [(srcn, 0, NE)], [(dn, lo, NE)])
                    tt_op("dve", sview(dst)[:, :, lo:19], sview(src)[:, :, lo:19], sview(src)[:, :, lo - sh:19 - sh], ALU.add,
                          [(srcn, ESO, ESO + NS * 19)], [(dn, ESO, ESO + NS * 19)])
                    src, srcn = dst, dn
                    sh *= 2
                    step += 1
                Fb, Fn = src, srcn
                tt_op("dve", Fb[:, EO:EO + 16], Fb[:, EO:EO + 16], FIX[:, g * 16:(g + 1) * 16], ALU.mult,
                      [(Fn, EO, EO + 16), ("FIX", 0, 64)], [(Fn, EO, EO + 16)])
                stt_op("dve", Hc(c, 0, PP), Fb[:, EO:EO + PP], 1.0 / w, E[:, EO:EO + PP], ALU.mult, ALU.subtract,
                       [(Fn, EO, EO + PP), ("TA", EO, EO + PP)], [kH(c, 0, PP)])
                stt_op("dve", Hc(c, S0, SE).rearrange("p (s j) -> p s j", j=4), sview(Fb)[:, :, 15:19], 1.0 / w, Es[:, :, 15:19],
                       ALU.mult, ALU.subtract, [(Fn, ESO, ESO + NS * 19), ("TA", ESO, ESO + NS * 19)], [kH(c, S0, SE)])
                memset_op("pool", Hc(c, SE, T), 0.0, [kH(c, SE, T)])
                state_out(E[:, EO + PP - 15:EO + PP], 15, Es[:, :, 15:19].rearrange("p s t -> p t s"), 4, ("TA", 0, 1408),
                          npp[o_, :, c * 128:(c + 1) * 128],
                          lambda t, c=c: nps[o_].rearrange("(s r) n -> s r n", r=15)[:, 11 + t, c * 128:(c + 1) * 128])
                if c % 4 == 3:
                    for m in range(4):
                        slot = mm_job(lambda k, tt, g=g: Hc(g * 4 + k, tt * TT, (tt + 1) * TT),
                                      lambda k, tt, g=g: kH(g * 4 + k, tt * TT, (tt + 1) * TT))
                        xc = g * 4 + m
                        for tt in range(NTT):
                            xv = Xc(xc, tt * TT, (tt + 1) * TT)
                            stt_op("dve", xv, ps_tile(slot, tt), ccol("psc", o_ * 16 + xc), xv, ALU.mult, ALU.add,
                                   [kps(slot, tt), kc(), kX(xc, tt * TT, (tt + 1) * TT)], [kX(xc, tt * TT, (tt + 1) * TT)])
                    for m in range(4):
                        nchunk(("gmlp", l, True), g * 4 + m)
            nfinish()

        nlayers = min(stage, 4)
        rms_prep(mix_spec(0))
        for l in range(nlayers):
            if l % 2 == 0:
                even_mixer(l)
            else:
                odd_mixer(l)
            mlp(l)

        if "norms" in DBG:
            pass
        for c in range(0 if "norms" in DBG else KC):
            stt_op("dve", Xc(c), Xc(c), ccol("gfin", c), RSTD[:, :], ALU.mult, ALU.mult, [kX(c), kc(), ("RSTD", 0, T)], [kX(c)])
        for blk in range(T // 128):
            si = blk % 4
            for cg in range(4):
                b = (blk * 4 + cg) % 8
                for j in range(4):
                    c = cg * 4 + j
                    P.add("pe", lambda e, b=b, j=j, c=c, blk=blk: e.transpose(PS[b][:, j * 128:(j + 1) * 128], Xc(c, blk * 128, (blk + 1) * 128), IDENT[:, :]),
                          [kX(c, blk * 128, (blk + 1) * 128), ("IDENT", 0, 128)], [(f"PS{b}", j * 128, (j + 1) * 128)])
                copy_op(alt_eng(), stage4(si)[:, cg * 512:(cg + 1) * 512], PS[b][:, :], [(f"PS{b}", 0, 512)], [kstage4(si, cg * 512, (cg + 1) * 512)])
            sp_dma(yout[blk * 128:(blk + 1) * 128, :], stage4(si), [kstage4(si)], ())

        P.finalize()
        P.emit(nc, sems, lane_sems)
    return nc


_W_NAMES = ["norm_mix", "norm_mlp", "norm_final", "w_in_even", "conv_a_w", "conv_a_b", "ln_a_g", "ln_a_b",
            "conv_b_w", "w_out_even", "pool_w", "pool_scale", "w_mlp_up", "w_mlp_down"]


def make_in_maps(inputs):
    xp = np.asarray(inputs["x_prompt"], np.float32)
    xs = np.asarray(inputs["x_sample"], np.float32)
    meta = np.asarray(inputs["meta_tokens"], np.float32)
    sa = np.asarray(inputs["state_conv_a"], np.float32)
    sb = np.asarray(inputs["state_conv_b"], np.float32)
    spl = np.asarray(inputs["state_pool"], np.float32)
    wts = {k: np.ascontiguousarray(np.asarray(inputs[k], np.float32)) for k in _W_NAMES}
    if int(os.environ.get("K_STAGE", "99")) == 0:
        for k in ("w_in_even", "w_out_even", "pool_w", "w_mlp_up", "w_mlp_down"):
            wts[k] = np.zeros((1, 1), np.float32)
    in_maps = []
    for c in range(8):
        b, half = c // 2, c % 2
        xext = np.concatenate([meta, xp[b]], axis=0)
        start = 0 if half == 0 else 2064 - PP
        xin = np.zeros((T, D), np.float32)
        xin[0:PP] = xext[start:start + PP]
        xin[S0:SE] = xs[c * NS:(c + 1) * NS].reshape(NS * LS, D)
        posf = np.broadcast_to((start + np.arange(16, dtype=np.float32))[None, :], (128, 16)).copy()
        m = {"xin": xin, "posf": posf,
             "sa": np.ascontiguousarray(sa[:, c * NS:(c + 1) * NS].reshape(2, NS * 30, DA)),
             "sb": np.ascontiguousarray(sb[:, c * NS:(c + 1) * NS].reshape(2, NS * 2, DA)),
             "spl": np.ascontiguousarray(spl[:, c * NS:(c + 1) * NS].reshape(2, NS * 15, D))}
        m.update(wts)
        in_maps.append(m)
    return in_maps


def gather(results):
    y_prompt = np.zeros((4, 2048, D), np.float32)
    y_sample = np.zeros((128, 4, D), np.float32)
    nca_p = np.zeros((2, 4, 30, DA), np.float32)
    ncb_p = np.zeros((2, 4, 2, DA), np.float32)
    npl_p = np.zeros((2, 4, 15, D), np.float32)
    nca_s = np.zeros((2, 128, 30, DA), np.float32)
    ncb_s = np.zeros((2, 128, 2, DA), np.float32)
    npl_s = np.zeros((2, 128, 15, D), np.float32)
    for c in range(8):
        r = results[c]
        b, half = c // 2, c % 2
        yo = np.asarray(r["yout"])
        if half == 0:
            y_prompt[b, 0:PP - 16] = yo[16:PP]
        else:
            y_prompt[b, PP - 16:] = yo[HALO:PP]
            nca_p[:, b] = np.asarray(r["nap"])
            ncb_p[:, b] = np.asarray(r["nbp"])
            npl_p[:, b] = np.asarray(r["npp"])
        y_sample[c * NS:(c + 1) * NS] = yo[S0:SE].reshape(NS, LS, D)
        nca_s[:, c * NS:(c + 1) * NS] = np.asarray(r["nas"]).reshape(2, NS, 30, DA)
        ncb_s[:, c * NS:(c + 1) * NS] = np.asarray(r["nbs"]).reshape(2, NS, 2, DA)
        npl_s[:, c * NS:(c + 1) * NS] = np.asarray(r["nps"]).reshape(2, NS, 15, D)
    return (y_prompt, y_sample, nca_p, ncb_p, npl_p, nca_s, ncb_s, npl_s)


_NC_CACHE = {}


def kernel(**inputs):
    stage = int(os.environ.get("K_STAGE", "99"))
    if stage not in _NC_CACHE:
        _NC_CACHE[stage] = build_nc(stage)
    nc = _NC_CACHE[stage]
    in_maps = make_in_maps(inputs)
    res = run_bass_kernel_spmd(nc, in_maps, core_ids=list(range(8)))
    return gather(res.results)
```

```python
import os
import contextlib
import numpy as np
import concourse.bass as bass
import concourse.mybir as mybir
from concourse.bass_utils import run_bass_kernel_spmd

F32 = mybir.dt.float32
BF16 = mybir.dt.bfloat16
AF = mybir.ActivationFunctionType
ALU = mybir.AluOpType

D = 2048
KC = 16
T = 1152
TT = 384
NTT = 3
PP = 1077
HALO = 90
NS = 16
LS = 4
S0 = PP
SE = PP + NS * LS
DA = 1024
DFF = 8192
EPS = 1e-6
NW = 4
NLANE = 8
HAZ = int(os.environ.get("K_HAZ", "1024"))
SAME_SYNC_ENGS = set(os.environ.get("K_SAME_SYNC", "act,pool").split(","))


class Op:
    __slots__ = ("eng", "fn", "deps", "signal", "sigval", "lane", "laneval", "idx", "size", "cum")

    def __init__(self, eng, fn):
        self.eng = eng
        self.fn = fn
        self.deps = set()
        self.signal = False
        self.sigval = 0
        self.lane = None
        self.laneval = 0
        self.idx = 0
        self.size = 0
        self.cum = 0


class Prog:
    ENGS = ("pe", "act", "dve", "pool", "sp")

    def __init__(self):
        self.ops = []
        self.acc = {}
        self.lane_cnt = {}
        self.cum = {}

    def add(self, eng, fn, reads=(), writes=(), lane=None, size=0):
        op = Op(eng, fn)
        op.idx = len(self.ops)
        op.size = size
        c = self.cum.get(eng, 0)
        op.cum = c
        self.cum[eng] = c + size + 60
        deps = op.deps
        for (k, lo, hi) in reads:
            a = self.acc.get(k)
            if a is None:
                continue
            for (l2, h2, p) in a[0]:
                if l2 < hi and lo < h2:
                    deps.add(p)
        for (k, lo, hi) in writes:
            a = self.acc.get(k)
            if a is None:
                continue
            for (l2, h2, p) in a[0]:
                if l2 < hi and lo < h2:
                    deps.add(p)
            for (l2, h2, p) in a[1]:
                if l2 < hi and lo < h2:
                    deps.add(p)
        isdma = lane is not None
        for (k, lo, hi) in reads:
            a = self.acc.setdefault(k, [[], []])
            if not isdma:
                a[1] = [e for e in a[1] if not (e[2].lane is None and e[2].eng == eng and lo <= e[0] and e[1] <= hi)]
            a[1].append((lo, hi, op))
        for (k, lo, hi) in writes:
            a = self.acc.setdefault(k, [[], []])
            a[0] = [e for e in a[0] if not (lo <= e[0] and e[1] <= hi)]
            a[1] = [e for e in a[1] if not (lo <= e[0] and e[1] <= hi)]
            a[0].append((lo, hi, op))
        if isdma:
            n = self.lane_cnt.get(lane, 0) + 1
            self.lane_cnt[lane] = n
            op.lane = lane
            op.laneval = 16 * n
        self.ops.append(op)
        return op

    @staticmethod
    def need_same_sync(op, d):
        if op.eng == "pe":
            return False
        if op.lane is not None or op.eng in SAME_SYNC_ENGS:
            return True
        if op.eng == "dve":
            gap = d.size + (op.cum - d.cum - d.size - 60)
            return gap < HAZ
        return False

    def finalize(self):
        for op in self.ops:
            for d in op.deps:
                if d.lane is not None:
                    continue
                if d.eng == op.eng and not self.need_same_sync(op, d):
                    continue
                d.signal = True
        cnt = {e: 0 for e in self.ENGS}
        for op in self.ops:
            if op.signal:
                cnt[op.eng] += 1
                op.sigval = cnt[op.eng]

    def schedule(self):
        per = {e: [o for o in self.ops if o.eng == e] for e in self.ENGS}
        final_lanes = dict(self.lane_cnt)
        acts = {}
        for ename in self.ENGS:
            lst = []
            waited = {}

            def wait(sem, val):
                if waited.get(sem, 0) < val:
                    lst.append(("wait", sem, val))
                    waited[sem] = val

            for op in per[ename]:
                if op.lane is not None and op.laneval > 16:
                    wait("L" + op.lane, op.laneval - 16)
                for d in sorted(op.deps, key=lambda o: o.idx):
                    if d.lane is not None:
                        wait("L" + d.lane, d.laneval)
                    else:
                        if d.eng == ename and not self.need_same_sync(op, d):
                            continue
                        wait("E" + d.eng, d.sigval)
                if op.lane is not None:
                    lst.append(("op", op, "L" + op.lane, 16))
                elif op.signal:
                    lst.append(("op", op, "E" + ename, 1))
                else:
                    lst.append(("op", op, None, 0))
            if ename == "sp":
                for ln, n in final_lanes.items():
                    wait("L" + ln, 16 * n)
                for en in self.ENGS:
                    last = [o for o in per[en] if o.signal]
                    if last and en != ename:
                        wait("E" + en, last[-1].sigval)
            acts[ename] = lst
        return acts

    def simulate(self, acts):
        semv = {}
        pc = {e: 0 for e in self.ENGS}
        progress = True
        while progress:
            progress = False
            for e in self.ENGS:
                lst = acts[e]
                while pc[e] < len(lst):
                    a = lst[pc[e]]
                    if a[0] == "wait":
                        if semv.get(a[1], 0) < a[2]:
                            break
                    else:
                        if a[2] is not None:
                            semv[a[2]] = semv.get(a[2], 0) + a[3]
                    pc[e] += 1
                    progress = True
        stuck = {e: (pc[e], len(acts[e]), acts[e][pc[e]] if pc[e] < len(acts[e]) else None) for e in self.ENGS}
        ok = all(pc[e] == len(acts[e]) for e in self.ENGS)
        return ok, stuck, semv

    def emit(self, nc, sems, lane_sems):
        acts = self.schedule()
        ok, stuck, semv = self.simulate(acts)
        if not ok:
            raise RuntimeError(f"sync deadlock in schedule: {stuck}")

        def semof(name):
            return lane_sems[name[1:]] if name[0] == "L" else sems[name[1:]]

        def body(ename):
            def run(e):
                for a in acts[ename]:
                    if a[0] == "wait":
                        e.wait_ge(semof(a[1]), a[2])
                    else:
                        ins = a[1].fn(e)
                        if a[2] is not None:
                            ins.then_inc(semof(a[2]), a[3])
            return run

        with nc.Block() as block:
            block.tensor(body("pe"))
            block.scalar(body("act"))
            block.vector(body("dve"))
            block.gpsimd(body("pool"))
            block.sync(body("sp"))


def build_nc(stage=99):
    nc = bass.Bass("TRN2", target_bir_lowering=False)
    P = Prog()

    def din(name, shape):
        return nc.dram_tensor(name, list(shape), F32, kind="ExternalInput").ap()

    def dout(name, shape):
        return nc.dram_tensor(name, list(shape), F32, kind="ExternalOutput").ap()

    xin = din("xin", [T, D])
    posf = din("posf", [128, 16])
    sa = din("sa", [2, NS * 30, DA])
    sb = din("sb", [2, NS * 2, DA])
    spl = din("spl", [2, NS * 15, D])
    norm_mix = din("norm_mix", [4, D])
    norm_mlp = din("norm_mlp", [4, D])
    norm_final = din("norm_final", [D])
    big = stage > 0
    w_in = din("w_in_even", [2, D, 5120] if big else [1, 1])
    conv_a_w = din("conv_a_w", [2, 31, DA])
    conv_a_b = din("conv_a_b", [2, DA])
    ln_a_g = din("ln_a_g", [2, DA])
    ln_a_b = din("ln_a_b", [2, DA])
    conv_b_w = din("conv_b_w", [2, 3, DA])
    w_out = din("w_out_even", [2, D, D] if big else [1, 1])
    pool_w = din("pool_w", [2, 4, 512, 512] if big else [1, 1])
    pool_scale = din("pool_scale", [2, D])
    w_up = din("w_mlp_up", [4, D, DFF] if big else [1, 1])
    w_down = din("w_mlp_down", [4, DFF, D] if big else [1, 1])

    yout = dout("yout", [T, D])
    nap = dout("nap", [2, 30, DA])
    nbp = dout("nbp", [2, 2, DA])
    npp = dout("npp", [2, 15, D])
    nas = dout("nas", [2, NS * 30, DA])
    nbs = dout("nbs", [2, NS * 2, DA])
    nps = dout("nps", [2, NS * 15, D])

    es = contextlib.ExitStack()
    with es:
        def sb_t(name, n, dt=F32):
            return es.enter_context(nc.sbuf_tensor(name, [128, n], dt))

        X = sb_t("X", KC * T)
        H = sb_t("H", KC * T, BF16)
        R1 = sb_t("R1", KC * T, BF16)
        WS = [sb_t(f"W{i}", 16 * 128, BF16) for i in range(NW)]
        RSTD = sb_t("RSTD", T)
        R2 = sb_t("R2", T)
        S1 = sb_t("S1", T)
        S2 = sb_t("S2", T)
        TA = sb_t("TA", 1408)
        TB = sb_t("TB", 1664)
        TC = sb_t("TC", 1408)
        CONST = sb_t("CONST", 768)
        IDENT = sb_t("IDENT", 128)
        ONES = sb_t("ONES", 128)
        FIX = sb_t("FIX", 64)
        POSF = sb_t("POSF", 16)
        DG = [sb_t(f"DG{i}", 128) for i in range(3)]
        STG = [sb_t(f"STG{i}", 512) for i in range(2)]
        GS = [sb_t(f"GS{i}", 64) for i in range(2)]
        STO = [sb_t(f"STO{i}", 256) for i in range(2)]
        PS = [es.enter_context(nc.psum_tensor(f"ps{i}", [128, 512], F32)) for i in range(8)]
        sems = {e: es.enter_context(nc.semaphore(f"sem_{e}")) for e in Prog.ENGS}
        lane_sems = {}
        for i in range(NW):
            lane_sems[f"w{i}"] = es.enter_context(nc.semaphore(f"lw{i}"))
        for i in range(NLANE):
            lane_sems[f"s{i}"] = es.enter_context(nc.semaphore(f"ls{i}"))

        R1f = R1[:, :].bitcast(F32)

        sp_lane = [0]

        def sp_dma(out, in_, reads=(), writes=()):
            ln = f"s{sp_lane[0] % NLANE}"
            sp_lane[0] += 1
            return P.add("sp", lambda e, o=out, i=in_: e.dma_start(out=o, in_=i), reads, writes, lane=ln)

        def Xc(c, lo=0, hi=T):
            return X[:, c * T + lo:c * T + hi]

        def Hc(c, lo=0, hi=T):
            return H[:, c * T + lo:c * T + hi]

        def kX(c, lo=0, hi=T):
            return ("X", c * T + lo, c * T + hi)

        def kH(c, lo=0, hi=T):
            return ("H", c * T + lo, c * T + hi)

        def kR1b(lo, hi):
            return ("R1", lo, hi)

        def fsz(ap):
            n = 1
            for d_ in ap.shape[1:]:
                n *= d_
            return n

        def act_op(out, in_, func, reads, writes, scale=None, bias=None):
            kw = {}
            if scale is not None:
                kw["scale"] = scale
            if bias is not None:
                kw["bias"] = bias
            return P.add("act", lambda e: e.activation(out=out, in_=in_, func=func, **kw), reads, writes, size=fsz(out))

        def tt_op(eng, out, in0, in1, op, reads, writes):
            return P.add(eng, lambda e: e.tensor_tensor(out=out, in0=in0, in1=in1, op=op), reads, writes, size=fsz(out))

        def ts_op(eng, out, in0, s1, s2, op0, op1, reads, writes):
            if s2 is None:
                s2, op1 = 0.0, ALU.add
            return P.add(eng, lambda e: e.tensor_scalar(out=out, in0=in0, scalar1=s1, scalar2=s2, op0=op0, op1=op1), reads, writes, size=fsz(out))

        def stt_op(eng, out, in0, scalar, in1, op0, op1, reads, writes):
            return P.add(eng, lambda e: e.scalar_tensor_tensor(out=out, in0=in0, scalar=scalar, in1=in1, op0=op0, op1=op1), reads, writes, size=fsz(out))

        def copy_op(eng, out, in_, reads, writes):
            if eng == "act":
                return P.add("act", lambda e: e.copy(out=out, in_=in_), reads, writes, size=fsz(out))
            return P.add(eng, lambda e: e.tensor_copy(out=out, in_=in_), reads, writes, size=fsz(out))

        def memset_op(eng, ap, val, writes):
            return P.add(eng, lambda e: e.memset(ap, val), (), writes)

        def mm_op(out, lhsT, rhs, start, stop, reads, writes):
            return P.add("pe", lambda e: e.matmul(out, lhsT, rhs, start=start, stop=stop), reads, writes)

        def tr_op(out, in_, reads, writes, n):
            return P.add("pe", lambda e: e.transpose(out, in_, IDENT[0:n, 0:n] if False else IDENT[:, :]), reads, writes)

        job_ctr = [0]

        def next_slot():
            s = job_ctr[0] % 2
            job_ctr[0] += 1
            return s

        def ps_tile(slot, tt):
            return PS[slot * 3 + tt][:, 0:TT]

        def kps(slot, tt):
            return (f"PS{slot * 3 + tt}", 0, TT)

        misc_ctr = [0]

        def next_misc():
            b = 6 + (misc_ctr[0] % 2)
            misc_ctr[0] += 1
            return b

        units = []

        def plan_units():
            for l in range(4):
                if l % 2 == 0:
                    e = l // 2
                    wv = w_in[e].rearrange("(k p) n -> p k n", p=128)
                    order = []
                    for c in range(8):
                        order += [8 + c, c]
                    for c in range(8):
                        order += [24 + c, 32 + c, 16 + c]
                    for blk in order:
                        units.append((wv[:, :, blk * 128:(blk + 1) * 128], 16))
                    wo = w_out[e].rearrange("(k p) n -> p k n", p=128)
                    for m in range(16):
                        units.append((wo[:, :, m * 128:(m + 1) * 128], 16))
                else:
                    o = l // 2
                    for g in range(4):
                        pw = pool_w[o, g].rearrange("(k p) n -> p k n", p=128)
                        for m in range(4):
                            units.append((pw[:, :, m * 128:(m + 1) * 128], 4))
                wu = w_up[l].rearrange("(k p) n -> p k n", p=128)
                for gi in range(4):
                    for fj in range(16):
                        f = gi * 16 + fj
                        units.append((wu[:, :, f * 128:(f + 1) * 128], 16))
                    wd = w_down[l][gi * 2048:(gi + 1) * 2048, :].rearrange("(k p) n -> p k n", p=128)
                    for m in range(16):
                        units.append((wd[:, :, m * 128:(m + 1) * 128], 16))

        if big:
            plan_units()
        unit_issued = [0]
        unit_next = [0]

        def issue_unit(u):
            if u >= len(units):
                return
            src, nk = units[u]
            slot = u % NW
            dst = WS[slot][:, 0:nk * 128].rearrange("p (k n) -> p k n", n=128)
            P.add("pool", lambda e, o=dst, i=src: e.dma_start(out=o, in_=i), (), [(f"W{slot}", 0, nk * 128)], lane=f"w{slot}")

        def take_unit():
            u = unit_next[0]
            unit_next[0] += 1
            while unit_issued[0] < min(u + NW, len(units)):
                issue_unit(unit_issued[0])
                unit_issued[0] += 1
            slot = u % NW
            nk = units[u][1]
            return slot, nk

        def mm_job(rhs_fn, rhs_key_fn):
            wslot, nk = take_unit()
            slot = next_slot()
            W3 = WS[wslot][:, 0:nk * 128].rearrange("p (k n) -> p k n", n=128)
            for k in range(nk):
                for tt in range(NTT):
                    mm_op(ps_tile(slot, tt), W3[:, k, :], rhs_fn(k, tt), k == 0, k == nk - 1,
                          [(f"W{wslot}", k * 128, (k + 1) * 128), rhs_key_fn(k, tt)], [kps(slot, tt)])
            return slot

        cst_src = [
            ("gmix", norm_mix.rearrange("l (c p) -> (l c) p", p=128), 64),
            ("gmlp", norm_mlp.rearrange("l (c p) -> (l c) p", p=128), 64),
            ("gfin", norm_final.rearrange("(c p) -> c p", p=128), 16),
            ("caw", conv_a_w.rearrange("e k (c p) -> (e k c) p", p=128), 496),
            ("cab", conv_a_b.rearrange("e (c p) -> (e c) p", p=128), 16),
            ("lng", ln_a_g.rearrange("e (c p) -> (e c) p", p=128), 16),
            ("lnb", ln_a_b.rearrange("e (c p) -> (e c) p", p=128), 16),
            ("cbw", conv_b_w.rearrange("e k (c p) -> (e k c) p", p=128), 48),
            ("psc", pool_scale.rearrange("o (c p) -> (o c) p", p=128), 32),
        ]
        coff = {}
        row = 0
        for name, ap, n in cst_src:
            coff[name] = row
            r = 0
            while r < n:
                blk = (row + r) // 128
                p0 = (row + r) % 128
                cnt = min(n - r, 128 - p0)
                sp_dma(R1f[p0:p0 + cnt, blk * 128:(blk + 1) * 128], ap[r:r + cnt, :], (),
                       [("R1", blk * 512, (blk + 1) * 512)])
                r += cnt
            row += n
        assert row == 768
        memset_op("pool", ONES[:, :], 1.0, [("ONES", 0, 128)])
        P.add("pool", lambda e: e.affine_select(out=IDENT[:, :], in_=ONES[:, :], pattern=[[1, 128]],
                                                compare_op=ALU.is_equal, fill=0.0, base=0, channel_multiplier=-1),
              [("ONES", 0, 128)], [("IDENT", 0, 128)])
        for blk in range(6):
            b = next_misc()
            P.add("pe", lambda e, b=b, blk=blk: e.transpose(PS[b][:, 0:128], R1f[:, blk * 128:(blk + 1) * 128], IDENT[:, :]),
                  [("R1", blk * 512, (blk + 1) * 512), ("IDENT", 0, 128)], [(f"PS{b}", 0, 128)])
            copy_op("dve", CONST[:, blk * 128:(blk + 1) * 128], PS[b][:, 0:128], [(f"PS{b}", 0, 128)],
                    [("CONST", blk * 128, (blk + 1) * 128)])

        def ccol(name, idx):
            o = coff[name] + idx
            return CONST[:, o:o + 1]

        def kc():
            return ("CONST", 0, 768)

        sp_dma(POSF[:, :], posf[:, :], (), [("POSF", 0, 16)])
        for g in range(4):
            w = float(2 ** (g + 1))
            fx = FIX[:, g * 16:(g + 1) * 16]
            ts_op("dve", fx, POSF[:, :], 1.0, w, ALU.add, ALU.min, [("POSF", 0, 16)], [("FIX", g * 16, g * 16 + 16)])
            P.add("dve", lambda e, fx=fx: e.reciprocal(out=fx, in_=fx), [("FIX", g * 16, g * 16 + 16)], [("FIX", g * 16, g * 16 + 16)])
            ts_op("dve", fx, fx, w, None, ALU.mult, None, [("FIX", g * 16, g * 16 + 16)], [("FIX", g * 16, g * 16 + 16)])

        def stage4(i):
            return R1f[:, i * 2048:(i + 1) * 2048]

        def kstage4(i, lo=0, hi=2048):
            return ("R1", i * 8192 + lo * 4, i * 8192 + hi * 4)

        cp_alt = [0]

        def alt_eng():
            cp_alt[0] += 1
            return "act" if cp_alt[0] % 2 else "dve"

        for blk in range(T // 128):
            si = blk % 4
            sp_dma(stage4(si), xin[blk * 128:(blk + 1) * 128, :], (), [kstage4(si)])
            for cg in range(4):
                b = (blk * 4 + cg) % 8
                for j in range(4):
                    c = cg * 4 + j
                    P.add("pe", lambda e, b=b, j=j, c=c, si=si: e.transpose(PS[b][:, j * 128:(j + 1) * 128], stage4(si)[:, c * 128:(c + 1) * 128], IDENT[:, :]),
                          [kstage4(si, c * 128, (c + 1) * 128), ("IDENT", 0, 128)], [(f"PS{b}", j * 128, (j + 1) * 128)])
                outv = X[:, :].rearrange("p (c t) -> p c t", t=T)[:, cg * 4:(cg + 1) * 4, blk * 128:(blk + 1) * 128]
                inv = PS[b][:, :].rearrange("p (c t) -> p c t", t=128)
                copy_op(alt_eng(), outv, inv, [(f"PS{b}", 0, 512)], [kX(c, blk * 128, (blk + 1) * 128) for c in range(cg * 4, cg * 4 + 4)])

        DBG = os.environ.get("K_DBG", "")
        for e_ in range(0 if "nod2d" in DBG else 2):
            sp_dma(nas[e_].rearrange("(s r) c -> s r c", r=30)[:, 0:26, :],
                   sa[e_].rearrange("(s r) c -> s r c", r=30)[:, 4:30, :])
            sp_dma(nps[e_].rearrange("(s r) c -> s r c", r=15)[:, 0:11, :],
                   spl[e_].rearrange("(s r) c -> s r c", r=15)[:, 4:15, :])

        def nchunk(spec, c, add_eng="dve"):
            gname, gidx, make_h = spec
            if c == 0:
                act_op(S2[:, :], Xc(c), AF.Square, [kX(c)], [("S2", 0, T)])
            else:
                tmp, tk = (S1, "S1") if c % 2 else (R2, "R2")
                act_op(tmp[:, 0:T], Xc(c), AF.Square, [kX(c)], [(tk, 0, T)])
                tt_op(add_eng, S2[:, :], S2[:, :], tmp[:, 0:T], ALU.add, [("S2", 0, T), (tk, 0, T)], [("S2", 0, T)])
            if make_h:
                act_op(Hc(c), Xc(c), AF.Copy, [kX(c), kc()], [kH(c)], scale=ccol(gname, gidx * 16 + c))

        def nfinish():
            slot = next_slot()
            for tt in range(NTT):
                mm_op(ps_tile(slot, tt), ONES[:, :], S2[:, tt * TT:(tt + 1) * TT], True, True,
                      [("ONES", 0, 128), ("S2", tt * TT, (tt + 1) * TT)], [kps(slot, tt)])
                rs = RSTD[:, tt * TT:(tt + 1) * TT]
                ts_op("dve", rs, ps_tile(slot, tt), 1.0 / D, EPS, ALU.mult, ALU.add, [kps(slot, tt)], [("RSTD", tt * TT, (tt + 1) * TT)])
                act_op(rs, rs, AF.Sqrt, [("RSTD", tt * TT, (tt + 1) * TT)], [("RSTD", tt * TT, (tt + 1) * TT)])
                P.add("dve", lambda e, rs=rs: e.reciprocal(out=rs, in_=rs), [("RSTD", tt * TT, (tt + 1) * TT)], [("RSTD", tt * TT, (tt + 1) * TT)])

        def rms_prep(spec):
            for c in range(KC):
                nchunk(spec, c)
            nfinish()

        def mix_spec(l):
            if l >= nlayers:
                return ("gfin", 0, False)
            return ("gmix", l, l % 2 == 0)

        def mlp(l):
            nxt = mix_spec(l + 1)
            for gi in range(4):
                for fj in range(16):
                    slot = mm_job(lambda k, tt: Hc(k, tt * TT, (tt + 1) * TT), lambda k, tt: kH(k, tt * TT, (tt + 1) * TT))
                    tmp, tk = (TA, "TA") if fj % 2 else (TC, "TC")
                    for tt in range(NTT):
                        tv = tmp[:, tt * TT:(tt + 1) * TT]
                        stt_op("dve", tv, ps_tile(slot, tt), 0.0, RSTD[:, tt * TT:(tt + 1) * TT], ALU.max, ALU.mult,
                               [kps(slot, tt), ("RSTD", tt * TT, (tt + 1) * TT)], [(tk, tt * TT, (tt + 1) * TT)])
                        lo = fj * T + tt * TT
                        act_op(R1[:, lo:lo + TT], tv, AF.Square, [(tk, tt * TT, (tt + 1) * TT)], [kR1b(lo * 2, (lo + TT) * 2)])
                for m in range(16):
                    slot = mm_job(lambda k, tt: R1[:, k * T + tt * TT:k * T + (tt + 1) * TT],
                                  lambda k, tt: kR1b((k * T + tt * TT) * 2, (k * T + (tt + 1) * TT) * 2))
                    for tt in range(NTT):
                        xv = Xc(m, tt * TT, (tt + 1) * TT)
                        tt_op("dve", xv, ps_tile(slot, tt), xv, ALU.add, [kps(slot, tt), kX(m, tt * TT, (tt + 1) * TT)],
                              [kX(m, tt * TT, (tt + 1) * TT)])
                    if gi == 3:
                        nchunk(nxt, m)
            nfinish()

        sto_ctr = [0]

        def state_out_prep(in_s, nt, key_in):
            i = sto_ctr[0] % 2
            sto_ctr[0] += 1
            gs = GS[i]
            copy_op("dve", gs[:, 0:nt * 16].rearrange("p (t s) -> p t s", s=16), in_s, [key_in], [(f"GS{i}", 0, 64)])
            return i

        def state_out_fin(i, in_p, np_, nt, key_in, dst_p, dst_s_fn):
            b = next_misc()
            sto = STO[i]
            gs = GS[i]
            P.add("pe", lambda e: e.transpose(PS[b][0:np_, 0:128], in_p, IDENT[:, :]), [key_in, ("IDENT", 0, 128)], [(f"PS{b}", 0, 256)])
            P.add("pe", lambda e: e.transpose(PS[b][0:nt * 16, 128:256], gs[:, 0:nt * 16], IDENT[:, :]), [(f"GS{i}", 0, 64), ("IDENT", 0, 128)], [(f"PS{b}", 0, 256)])
            copy_op("act", sto[0:np_, 0:128], PS[b][0:np_, 0:128], [(f"PS{b}", 0, 128)], [(f"STO{i}", 0, 128)])
            copy_op("act", sto[0:nt * 16, 128:256], PS[b][0:nt * 16, 128:256], [(f"PS{b}", 128, 256)], [(f"STO{i}", 128, 256)])
            sp_dma(dst_p, sto[0:np_, 0:128], [(f"STO{i}", 0, 128)], ())
            for t in range(nt):
                sp_dma(dst_s_fn(t), sto[t * 16:(t + 1) * 16, 128:256], [(f"STO{i}", 128, 256)], ())

        def state_out(in_p, np_, in_s, nt, key_in, dst_p, dst_s_fn):
            i = state_out_prep(in_s, nt, key_in)
            state_out_fin(i, in_p, np_, nt, key_in, dst_p, dst_s_fn)

        stg_ctr = [0]

        def state_load(src_rows, nblk, rows_per_blk):
            i = stg_ctr[0] % 2
            stg_ctr[0] += 1
            stg = STG[i]
            for q in range(nblk):
                sp_dma(stg[0:rows_per_blk, q * 128:(q + 1) * 128], src_rows[q * rows_per_blk:(q + 1) * rows_per_blk, :], (),
                       [(f"STG{i}", q * 128, (q + 1) * 128)])
            return i

        def state_xpose(i, nblk, rows_per_blk, dst3, dkeys, hist):
            stg = STG[i]
            b = next_misc()
            for q in range(nblk):
                P.add("pe", lambda e, q=q: e.transpose(PS[b][:, q * rows_per_blk:(q + 1) * rows_per_blk], stg[0:rows_per_blk, q * 128:(q + 1) * 128], IDENT[0:rows_per_blk, 0:rows_per_blk]),
                      [(f"STG{i}", q * 128, (q + 1) * 128), ("IDENT", 0, 128)], [(f"PS{b}", q * rows_per_blk, (q + 1) * rows_per_blk)])
            n = nblk * rows_per_blk
            copy_op("act", dst3, PS[b][:, 0:n].rearrange("p (s r) -> p s r", r=hist), [(f"PS{b}", 0, n)], dkeys)

        def state_in(src_rows, nblk, rows_per_blk, dst3, dkeys, hist):
            i = state_load(src_rows, nblk, rows_per_blk)
            state_xpose(i, nblk, rows_per_blk, dst3, dkeys, hist)

        def even_mixer(l):
            e_ = l // 2
            AOFF = 30
            ASO = 30 + PP
            TBs = TB[:, ASO:ASO + NS * 34].rearrange("p (s j) -> p s j", j=34)
            kTB = ("TB", 0, 1664)
            memset_op("pool", TB[:, 0:30], 0.0, [("TB", 0, 30)])
            sav = sa[e_]
            Hrhs = (lambda k, tt: Hc(k, tt * TT, (tt + 1) * TT), lambda k, tt: kH(k, tt * TT, (tt + 1) * TT))

            def a_jobs_gate():
                s_gate = mm_job(*Hrhs)
                for tt in range(NTT):
                    copy_op("act", TA[:, tt * TT:(tt + 1) * TT], ps_tile(s_gate, tt), [kps(s_gate, tt)], [("TA", tt * TT, (tt + 1) * TT)])

            def a_gate_sig():
                tt_op("dve", TA[:, 0:T], TA[:, 0:T], RSTD[:, :], ALU.mult, [("TA", 0, T), ("RSTD", 0, T)], [("TA", 0, T)])
                act_op(TA[:, 0:T], TA[:, 0:T], AF.Sigmoid, [("TA", 0, T)], [("TA", 0, T)])

            NPE = int(os.environ.get("K_NPE", "10"))
            dg_ctr = [0]

            def conv_pe(c):
                slot = next_slot()
                taps = list(range(31 - NPE, 31))
                for j, k in enumerate(taps):
                    di = dg_ctr[0] % 3
                    dg_ctr[0] += 1
                    wk = ccol("caw", (e_ * 31 + k) * 8 + c)
                    act_op(DG[di][:, :], IDENT[:, :], AF.Copy, [("IDENT", 0, 128), kc()], [(f"DG{di}", 0, 128)], scale=wk)
                    for tt in range(NTT):
                        ncol = TT if tt < 2 else PP - 2 * TT
                        mm_op(PS[slot * 3 + tt][:, 0:ncol], DG[di][:, :], TB[:, k + tt * TT:k + tt * TT + ncol], j == 0, j == len(taps) - 1,
                              [(f"DG{di}", 0, 128), kTB], [kps(slot, tt)])
                return slot

            ld = state_load(sav[:, 0:128], 4, 120)
            a_jobs_gate()
            a_gate_sig()
            s_val = mm_job(*Hrhs)
            for c in range(8):
                ld_next = state_load(sav[:, (c + 1) * 128:(c + 2) * 128], 4, 120) if c + 1 < 8 else None
                state_xpose(ld, 4, 120, TBs[:, :, 0:30], [("TB", ASO, ASO + NS * 34)], 30)
                ld = ld_next
                tt_op("dve", TA[:, 0:T], TA[:, 0:T], RSTD[:, :], ALU.mult, [("TA", 0, T), ("RSTD", 0, T)], [("TA", 0, T)])
                for tt in range(2):
                    tt_op("dve", TB[:, AOFF + tt * TT:AOFF + (tt + 1) * TT], ps_tile(s_val, tt), TA[:, tt * TT:(tt + 1) * TT], ALU.mult,
                          [kps(s_val, tt), ("TA", tt * TT, (tt + 1) * TT)], [("TB", AOFF + tt * TT, AOFF + (tt + 1) * TT)])
                np2 = PP - 2 * TT
                tt_op("dve", TB[:, AOFF + 2 * TT:AOFF + PP], PS[s_val * 3 + 2][:, 0:np2], TA[:, 2 * TT:PP], ALU.mult,
                      [kps(s_val, 2), ("TA", 2 * TT, PP)], [("TB", AOFF + 2 * TT, AOFF + PP)])
                tt_op("dve", TBs[:, :, 30:34], PS[s_val * 3 + 2][:, np2:np2 + 64].rearrange("p (s j) -> p s j", j=4),
                      TA[:, S0:SE].rearrange("p (s j) -> p s j", j=4), ALU.mult,
                      [kps(s_val, 2), ("TA", S0, SE)], [("TB", ASO, ASO + NS * 34)])
                state_out(TB[:, AOFF + PP - 30:AOFF + PP], 30, TBs[:, :, 30:34].rearrange("p s t -> p t s"), 4, kTB,
                          nap[e_, :, c * 128:(c + 1) * 128],
                          lambda t, c=c: nas[e_].rearrange("(s r) n -> s r n", r=30)[:, 26 + t, c * 128:(c + 1) * 128])
                if c + 1 < 8:
                    a_jobs_gate()
                    s_val_next = mm_job(*Hrhs)
                s_cv = conv_pe(c) if NPE > 0 else None
                acc = R1f[:, c * T:(c + 1) * T]
                kacc = kR1b(c * T * 4, (c + 1) * T * 4)
                accs = acc[:, S0:SE].rearrange("p (s j) -> p s j", j=4)
                memset_op("pool", acc[:, SE:T], 0.0, [kR1b((c * T + SE) * 4, (c + 1) * T * 4)])
                for k in range(31):
                    wk = ccol("caw", (e_ * 31 + k) * 8 + c)
                    if k == 0:
                        ts_op("dve", acc[:, 0:PP], TB[:, 0:PP], wk, ccol("cab", e_ * 8 + c), ALU.mult, ALU.add,
                              [kTB, kc()], [kR1b(c * T * 4, (c * T + PP) * 4)])
                        ts_op("dve", accs, TBs[:, :, 0:4], wk, ccol("cab", e_ * 8 + c), ALU.mult, ALU.add,
                              [kTB, kc()], [kR1b((c * T + S0) * 4, (c * T + SE) * 4)])
                    else:
                        if k < 31 - NPE:
                            stt_op("dve", acc[:, 0:PP], TB[:, k:k + PP], wk, acc[:, 0:PP], ALU.mult, ALU.add,
                                   [kTB, kc(), kR1b(c * T * 4, (c * T + PP) * 4)], [kR1b(c * T * 4, (c * T + PP) * 4)])
                        stt_op("dve", accs, TBs[:, :, k:k + 4], wk, accs, ALU.mult, ALU.add,
                               [kTB, kc(), kR1b((c * T + S0) * 4, (c * T + SE) * 4)], [kR1b((c * T + S0) * 4, (c * T + SE) * 4)])
                if c + 1 < 8:
                    a_gate_sig()
                if s_cv is not None:
                    for tt in range(NTT):
                        ncol = TT if tt < 2 else PP - 2 * TT
                        av = acc[:, tt * TT:tt * TT + ncol]
                        ka = kR1b((c * T + tt * TT) * 4, (c * T + tt * TT + ncol) * 4)
                        tt_op("dve", av, PS[s_cv * 3 + tt][:, 0:ncol], av, ALU.add, [kps(s_cv, tt), ka], [ka])
                if c == 0:
                    act_op(S2[:, :], acc, AF.Square, [kacc], [("S2", 0, T)])
                    copy_op("dve", S1[:, :], acc, [kacc], [("S1", 0, T)])
                else:
                    act_op(TC[:, 0:T], acc, AF.Square, [kacc], [("TC", 0, T)])
                    tt_op("pool", S1[:, :], S1[:, :], acc, ALU.add, [("S1", 0, T), kacc], [("S1", 0, T)])
                    tt_op("pool", S2[:, :], S2[:, :], TC[:, 0:T], ALU.add, [("S2", 0, T), ("TC", 0, T)], [("S2", 0, T)])
                if c + 1 < 8:
                    s_val = s_val_next
            s_1 = next_slot()
            for tt in range(NTT):
                mm_op(ps_tile(s_1, tt), ONES[:, :], S1[:, tt * TT:(tt + 1) * TT], True, True,
                      [("ONES", 0, 128), ("S1", tt * TT, (tt + 1) * TT)], [kps(s_1, tt)])
            s_2 = next_slot()
            for tt in range(NTT):
                mm_op(ps_tile(s_2, tt), ONES[:, :], S2[:, tt * TT:(tt + 1) * TT], True, True,
                      [("ONES", 0, 128), ("S2", tt * TT, (tt + 1) * TT)], [kps(s_2, tt)])
            for tt in range(NTT):
                sl = slice(tt * TT, (tt + 1) * TT)
                kM = ("R2", tt * TT, (tt + 1) * TT)
                kV = ("S2", tt * TT, (tt + 1) * TT)
                ts_op("dve", R2[:, sl], ps_tile(s_1, tt), 1.0 / DA, None, ALU.mult, None, [kps(s_1, tt)], [kM])
                tt_op("dve", S2[:, sl], R2[:, sl], R2[:, sl], ALU.mult, [kM], [kV])
                stt_op("dve", S2[:, sl], ps_tile(s_2, tt), 1.0 / DA, S2[:, sl], ALU.mult, ALU.subtract, [kps(s_2, tt), kV], [kV])
                ts_op("dve", S2[:, sl], S2[:, sl], 0.0, EPS, ALU.max, ALU.add, [kV], [kV])
                act_op(S2[:, sl], S2[:, sl], AF.Sqrt, [kV], [kV])
                P.add("dve", lambda e, v=S2[:, sl]: e.reciprocal(out=v, in_=v), [kV], [kV])
                stt_op("dve", R2[:, sl], R2[:, sl], -1.0, S2[:, sl], ALU.mult, ALU.mult, [kM, kV], [kM])

            def ln_apply(c):
                acc = R1f[:, c * T:(c + 1) * T]
                kacc = kR1b(c * T * 4, (c + 1) * T * 4)
                tt_op("dve", acc, acc, S2[:, 0:T], ALU.mult, [kacc, ("S2", 0, T)], [kacc])
                tt_op("dve", acc, acc, R2[:, 0:T], ALU.add, [kacc, ("R2", 0, T)], [kacc])
                act_op(R1[:, c * 2 * T:c * 2 * T + T], acc, AF.Silu, [kacc, kc()], [kacc],
                       scale=ccol("lng", e_ * 8 + c), bias=ccol("lnb", e_ * 8 + c))
            VOFF = 2
            VSO = 2 + PP
            TBv = TB[:, VSO:VSO + NS * 6].rearrange("p (s j) -> p s j", j=6)
            memset_op("pool", TB[:, 0:2], 0.0, [("TB", 0, 2)])
            memset_op("pool", S1[:, SE:T], 0.0, [("S1", SE, T)])
            sbv = sb[e_]
            ldb = state_load(sbv[:, 0:128], 1, 32)
            for c in range(8):
                ldb_next = state_load(sbv[:, (c + 1) * 128:(c + 2) * 128], 1, 32) if c + 1 < 8 else None
                state_xpose(ldb, 1, 32, TBv[:, :, 0:2], [("TB", VSO, VSO + NS * 6)], 2)
                ldb = ldb_next
                s_cg = mm_job(lambda k, tt: Hc(k, tt * TT, (tt + 1) * TT), lambda k, tt: kH(k, tt * TT, (tt + 1) * TT))
                ln_apply(c)
                for tt in range(NTT):
                    tt_op("dve", TA[:, tt * TT:(tt + 1) * TT], ps_tile(s_cg, tt), RSTD[:, tt * TT:(tt + 1) * TT], ALU.mult,
                          [kps(s_cg, tt), ("RSTD", tt * TT, (tt + 1) * TT)], [("TA", tt * TT, (tt + 1) * TT)])
                tt_op("dve", TA[:, 0:T], TA[:, 0:T], RSTD[:, :], ALU.mult, [("TA", 0, T), ("RSTD", 0, T)], [("TA", 0, T)])
                s_bx = mm_job(lambda k, tt: Hc(k, tt * TT, (tt + 1) * TT), lambda k, tt: kH(k, tt * TT, (tt + 1) * TT))
                for tt in range(2):
                    tt_op("dve", TB[:, VOFF + tt * TT:VOFF + (tt + 1) * TT], ps_tile(s_bx, tt), TA[:, tt * TT:(tt + 1) * TT], ALU.mult,
                          [kps(s_bx, tt), ("TA", tt * TT, (tt + 1) * TT)], [("TB", VOFF + tt * TT, VOFF + (tt + 1) * TT)])
                np2 = PP - 2 * TT
                tt_op("dve", TB[:, VOFF + 2 * TT:VOFF + PP], PS[s_bx * 3 + 2][:, 0:np2], TA[:, 2 * TT:PP], ALU.mult,
                      [kps(s_bx, 2), ("TA", 2 * TT, PP)], [("TB", VOFF + 2 * TT, VOFF + PP)])
                tt_op("dve", TBv[:, :, 2:6], PS[s_bx * 3 + 2][:, np2:np2 + 64].rearrange("p (s j) -> p s j", j=4),
                      TA[:, S0:SE].rearrange("p (s j) -> p s j", j=4), ALU.mult,
                      [kps(s_bx, 2), ("TA", S0, SE)], [("TB", VSO, VSO + NS * 6)])
                so_i = state_out_prep(TBv[:, :, 4:6].rearrange("p s t -> p t s"), 2, kTB)
                S1s = S1[:, S0:SE].rearrange("p (s j) -> p s j", j=4)
                for k in range(3):
                    wk = ccol("cbw", (e_ * 3 + k) * 8 + c)
                    if k == 0:
                        ts_op("dve", S1[:, 0:PP], TB[:, 0:PP], wk, None, ALU.mult, None, [kTB, kc()], [("S1", 0, PP)])
                        ts_op("dve", S1s, TBv[:, :, 0:4], wk, None, ALU.mult, None, [kTB, kc()], [("S1", S0, SE)])
                    else:
                        stt_op("dve", S1[:, 0:PP], TB[:, k:k + PP], wk, S1[:, 0:PP], ALU.mult, ALU.add, [kTB, kc(), ("S1", 0, PP)], [("S1", 0, PP)])
                        stt_op("dve", S1s, TBv[:, :, k:k + 4], wk, S1s, ALU.mult, ALU.add, [kTB, kc(), ("S1", S0, SE)], [("S1", S0, SE)])
                s_bg = mm_job(lambda k, tt: Hc(k, tt * TT, (tt + 1) * TT), lambda k, tt: kH(k, tt * TT, (tt + 1) * TT))
                for tt in range(NTT):
                    tt_op("dve", TC[:, tt * TT:(tt + 1) * TT], ps_tile(s_bg, tt), RSTD[:, tt * TT:(tt + 1) * TT], ALU.mult,
                          [kps(s_bg, tt), ("RSTD", tt * TT, (tt + 1) * TT)], [("TC", tt * TT, (tt + 1) * TT)])
                bo = c * 2 * T + T
                tt_op("dve", R1[:, bo:bo + T], TC[:, 0:T], S1[:, :], ALU.mult, [("TC", 0, T), ("S1", 0, T)], [kR1b(bo * 2, (bo + T) * 2)])
                state_out_fin(so_i, TB[:, VOFF + PP - 2:VOFF + PP], 2, 2, kTB,
                          nbp[e_, :, c * 128:(c + 1) * 128],
                          lambda t, c=c: nbs[e_].rearrange("(s r) n -> s r n", r=2)[:, t, c * 128:(c + 1) * 128])
            for m in range(16):
                def rhs(k, tt):
                    base = (k % 8) * 2 * T + (T if k >= 8 else 0)
                    return R1[:, base + tt * TT:base + (tt + 1) * TT]

                def rkey(k, tt):
                    base = (k % 8) * 2 * T + (T if k >= 8 else 0)
                    return kR1b((base + tt * TT) * 2, (base + (tt + 1) * TT) * 2)
                slot = mm_job(rhs, rkey)
                for tt in range(NTT):
                    xv = Xc(m, tt * TT, (tt + 1) * TT)
                    tt_op("dve", xv, ps_tile(slot, tt), xv, ALU.add, [kps(slot, tt), kX(m, tt * TT, (tt + 1) * TT)], [kX(m, tt * TT, (tt + 1) * TT)])
                nchunk(("gmlp", l, True), m)
            nfinish()

        def odd_mixer(l):
            o_ = l // 2
            EO = 15
            ESO = 15 + PP
            NE = ESO
            bufs = {"TA": TA, "TB": TB, "TC": TC}

            def sview(buf):
                return buf[:, ESO:ESO + NS * 19].rearrange("p (s j) -> p s j", j=19)
            for nm in ("TA", "TB", "TC"):
                memset_op("pool", bufs[nm][:, 0:15], 0.0, [(nm, 0, 15)])
            spv = spl[o_]
            ld = state_load(spv[:, 0:128], 2, 120)
            for c in range(KC):
                g = c // 4
                w = 2 ** (g + 1)
                E, Es = TA, sview(TA)
                ld_next = state_load(spv[:, (c + 1) * 128:(c + 2) * 128], 2, 120) if c + 1 < KC else None
                state_xpose(ld, 2, 120, Es[:, :, 0:15], [("TA", ESO, ESO + NS * 19)], 15)
                ld = ld_next
                gcol = ccol("gmix", l * 16 + c)
                stt_op("dve", E[:, EO:EO + PP], Xc(c, 0, PP), gcol, RSTD[:, 0:PP], ALU.mult, ALU.mult,
                       [kX(c, 0, PP), kc(), ("RSTD", 0, PP)], [("TA", EO, EO + PP)])
                stt_op("dve", Es[:, :, 15:19], Xc(c, S0, SE).rearrange("p (s j) -> p s j", j=4), gcol,
                       RSTD[:, S0:SE].rearrange("p (s j) -> p s j", j=4), ALU.mult, ALU.mult,
                       [kX(c, S0, SE), kc(), ("RSTD", S0, SE)], [("TA", ESO, ESO + NS * 19)])
                src, srcn = E, "TA"
                pp = [("TB", TB), ("TC", TC)]
                sh = 1
                step = 0
                while sh < w:
                    dn, dst = pp[step % 2]
                    lo = 2 * sh - 1
                    tt_op("dve", dst[:, lo:NE], src[:, lo:NE], src[:, lo - sh:NE - sh], ALU.add,
                          [(srcn, 0, NE)], [(dn, lo, NE)])
                    tt_op("dve", sview(dst)[:, :, lo:19], sview(src)[:, :, lo:19], sview(src)[:, :, lo - sh:19 - sh], ALU.add,
                          [(srcn, ESO, ESO + NS * 19)], [(dn, ESO, ESO + NS * 19)])
                    src, srcn = dst, dn
                    sh *= 2
                    step += 1
                Fb, Fn = src, srcn
                tt_op("dve", Fb[:, EO:EO + 16], Fb[:, EO:EO + 16], FIX[:, g * 16:(g + 1) * 16], ALU.mult,
                      [(Fn, EO, EO + 16), ("FIX", 0, 64)], [(Fn, EO, EO + 16)])
                stt_op("dve", Hc(c, 0, PP), Fb[:, EO:EO + PP], 1.0 / w, E[:, EO:EO + PP], ALU.mult, ALU.subtract,
                       [(Fn, EO, EO + PP), ("TA", EO, EO + PP)], [kH(c, 0, PP)])
                stt_op("dve", Hc(c, S0, SE).rearrange("p (s j) -> p s j", j=4), sview(Fb)[:, :, 15:19], 1.0 / w, Es[:, :, 15:19],
                       ALU.mult, ALU.subtract, [(Fn, ESO, ESO + NS * 19), ("TA", ESO, ESO + NS * 19)], [kH(c, S0, SE)])
                memset_op("pool", Hc(c, SE, T), 0.0, [kH(c, SE, T)])
                state_out(E[:, EO + PP - 15:EO + PP], 15, Es[:, :, 15:19].rearrange("p s t -> p t s"), 4, ("TA", 0, 1408),
                          npp[o_, :, c * 128:(c + 1) * 128],
                          lambda t, c=c: nps[o_].rearrange("(s r) n -> s r n", r=15)[:, 11 + t, c * 128:(c + 1) * 128])
                if c % 4 == 3:
                    for m in range(4):
                        slot = mm_job(lambda k, tt, g=g: Hc(g * 4 + k, tt * TT, (tt + 1) * TT),
                                      lambda k, tt, g=g: kH(g * 4 + k, tt * TT, (tt + 1) * TT))
                        xc = g * 4 + m
                        for tt in range(NTT):
                            xv = Xc(xc, tt * TT, (tt + 1) * TT)
                            stt_op("dve", xv, ps_tile(slot, tt), ccol("psc", o_ * 16 + xc), xv, ALU.mult, ALU.add,
                                   [kps(slot, tt), kc(), kX(xc, tt * TT, (tt + 1) * TT)], [kX(xc, tt * TT, (tt + 1) * TT)])
                    for m in range(4):
                        nchunk(("gmlp", l, True), g * 4 + m)
            nfinish()

        nlayers = min(stage, 4)
        rms_prep(mix_spec(0))
        for l in range(nlayers):
            if l % 2 == 0:
                even_mixer(l)
            else:
                odd_mixer(l)
            mlp(l)

        if "norms" in DBG:
            pass
        for c in range(0 if "norms" in DBG else KC):
            stt_op("dve", Xc(c), Xc(c), ccol("gfin", c), RSTD[:, :], ALU.mult, ALU.mult, [kX(c), kc(), ("RSTD", 0, T)], [kX(c)])
        for blk in range(T // 128):
            si = blk % 4
            for cg in range(4):
                b = (blk * 4 + cg) % 8
                for j in range(4):
                    c = cg * 4 + j
                    P.add("pe", lambda e, b=b, j=j, c=c, blk=blk: e.transpose(PS[b][:, j * 128:(j + 1) * 128], Xc(c, blk * 128, (blk + 1) * 128), IDENT[:, :]),
                          [kX(c, blk * 128, (blk + 1) * 128), ("IDENT", 0, 128)], [(f"PS{b}", j * 128, (j + 1) * 128)])
                copy_op(alt_eng(), stage4(si)[:, cg * 512:(cg + 1) * 512], PS[b][:, :], [(f"PS{b}", 0, 512)], [kstage4(si, cg * 512, (cg + 1) * 512)])
            sp_dma(yout[blk * 128:(blk + 1) * 128, :], stage4(si), [kstage4(si)], ())

        P.finalize()
        P.emit(nc, sems, lane_sems)
    return nc


_W_NAMES = ["norm_mix", "norm_mlp", "norm_final", "w_in_even", "conv_a_w", "conv_a_b", "ln_a_g", "ln_a_b",
            "conv_b_w", "w_out_even", "pool_w", "pool_scale", "w_mlp_up", "w_mlp_down"]


def make_in_maps(inputs):
    xp = np.asarray(inputs["x_prompt"], np.float32)
    xs = np.asarray(inputs["x_sample"], np.float32)
    meta = np.asarray(inputs["meta_tokens"], np.float32)
    sa = np.asarray(inputs["state_conv_a"], np.float32)
    sb = np.asarray(inputs["state_conv_b"], np.float32)
    spl = np.asarray(inputs["state_pool"], np.float32)
    wts = {k: np.ascontiguousarray(np.asarray(inputs[k], np.float32)) for k in _W_NAMES}
    if int(os.environ.get("K_STAGE", "99")) == 0:
        for k in ("w_in_even", "w_out_even", "pool_w", "w_mlp_up", "w_mlp_down"):
            wts[k] = np.zeros((1, 1), np.float32)
    in_maps = []
    for c in range(8):
        b, half = c // 2, c % 2
        xext = np.concatenate([meta, xp[b]], axis=0)
        start = 0 if half == 0 else 2064 - PP
        xin = np.zeros((T, D), np.float32)
        xin[0:PP] = xext[start:start + PP]
        xin[S0:SE] = xs[c * NS:(c + 1) * NS].reshape(NS * LS, D)
        posf = np.broadcast_to((start + np.arange(16, dtype=np.float32))[None, :], (128, 16)).copy()
        m = {"xin": xin, "posf": posf,
             "sa": np.ascontiguousarray(sa[:, c * NS:(c + 1) * NS].reshape(2, NS * 30, DA)),
             "sb": np.ascontiguousarray(sb[:, c * NS:(c + 1) * NS].reshape(2, NS * 2, DA)),
             "spl": np.ascontiguousarray(spl[:, c * NS:(c + 1) * NS].reshape(2, NS * 15, D))}
        m.update(wts)
        in_maps.append(m)
    return in_maps


def gather(results):
    y_prompt = np.zeros((4, 2048, D), np.float32)
    y_sample = np.zeros((128, 4, D), np.float32)
    nca_p = np.zeros((2, 4, 30, DA), np.float32)
    ncb_p = np.zeros((2, 4, 2, DA), np.float32)
    npl_p = np.zeros((2, 4, 15, D), np.float32)
    nca_s = np.zeros((2, 128, 30, DA), np.float32)
    ncb_s = np.zeros((2, 128, 2, DA), np.float32)
    npl_s = np.zeros((2, 128, 15, D), np.float32)
    for c in range(8):
        r = results[c]
        b, half = c // 2, c % 2
        yo = np.asarray(r["yout"])
        if half == 0:
            y_prompt[b, 0:PP - 16] = yo[16:PP]
        else:
            y_prompt[b, PP - 16:] = yo[HALO:PP]
            nca_p[:, b] = np.asarray(r["nap"])
            ncb_p[:, b] = np.asarray(r["nbp"])
            npl_p[:, b] = np.asarray(r["npp"])
        y_sample[c * NS:(c + 1) * NS] = yo[S0:SE].reshape(NS, LS, D)
        nca_s[:, c * NS:(c + 1) * NS] = np.asarray(r["nas"]).reshape(2, NS, 30, DA)
        ncb_s[:, c * NS:(c + 1) * NS] = np.asarray(r["nbs"]).reshape(2, NS, 2, DA)
        npl_s[:, c * NS:(c + 1) * NS] = np.asarray(r["nps"]).reshape(2, NS, 15, D)
    return (y_prompt, y_sample, nca_p, ncb_p, npl_p, nca_s, ncb_s, npl_s)


_NC_CACHE = {}


def kernel(**inputs):
    stage = int(os.environ.get("K_STAGE", "99"))
    if stage not in _NC_CACHE:
        _NC_CACHE[stage] = build_nc(stage)
    nc = _NC_CACHE[stage]
    in_maps = make_in_maps(inputs)
    res = run_bass_kernel_spmd(nc, in_maps, core_ids=list(range(8)))
    return gather(res.results)
```
